# Optimizing a Trainium2 kernel written in Bass

```python
import math
import jax, jax.numpy as jnp
from jax import lax
import numpy as np

D_MODEL = 1024
BATCH = 4
SEQ = 8192
DEPTH = 1

D_FF = 2816
FFN_RES = 0.5
GLA_HEADS = 4
GLA_DK = 128
GLA_DV = 256
GLA_RANK = 16
GLA_TAU = 16.0
DN_HEADS = 8
DN_DK = 128
DN_DV = 128
CONV_K = 4
CHUNK = 64
EPS = 1e-6

GLA_QK = GLA_HEADS * GLA_DK
GLA_V = GLA_HEADS * GLA_DV
DN_QK = DN_HEADS * DN_DK
DN_V = DN_HEADS * DN_DV
IN_SIZES = (GLA_QK, GLA_QK, GLA_V, GLA_V, GLA_RANK,
            DN_QK, DN_QK, DN_V, DN_V, DN_HEADS, DN_HEADS,
            D_MODEL, D_MODEL)
D_IN = sum(IN_SIZES)

kernel_name = "macaron_gla_gdn_gated_merge"


def rms_norm(x, w):
    xf = x.astype(jnp.float32)
    xf = xf * lax.rsqrt(jnp.mean(xf * xf, axis=-1, keepdims=True) + EPS)
    return xf.astype(x.dtype) * w


def head_rms_norm(o, w, heads):
    b, t, _ = o.shape
    oh = o.reshape(b, t, heads, -1)
    oh = oh * lax.rsqrt(jnp.mean(oh * oh, axis=-1, keepdims=True) + EPS) * w.astype(jnp.float32)
    return oh.reshape(b, t, -1)


def swiglu(h, w_gate, w_up, w_down):
    return (jax.nn.silu(h @ w_gate) * (h @ w_up)) @ w_down


def to_chunks(t, heads):
    b, T, _ = t.shape
    return t.reshape(b, T // CHUNK, CHUNK, heads, -1).transpose(0, 3, 1, 2, 4)


def scalar_chunks(t):
    b, T, h = t.shape
    return t.reshape(b, T // CHUNK, CHUNK, h).transpose(0, 3, 1, 2)


def from_chunks(t):
    b, h, n, c, d = t.shape
    return t.transpose(0, 2, 3, 1, 4).reshape(b, n * c, h * d)


def causal_depthwise_conv(x, w):
    c = x.shape[-1]
    return lax.conv_general_dilated(
        x, w[:, None, :], window_strides=(1,), padding=[(CONV_K - 1, 0)],
        dimension_numbers=("NWC", "WIO", "NWC"), feature_group_count=c)


def gla_attention(q, k, v, log_a):
    f32 = jnp.float32
    q = to_chunks(q.astype(f32), GLA_HEADS) * GLA_DK ** -0.5
    k = to_chunks(k.astype(f32), GLA_HEADS)
    v = to_chunks(v.astype(f32), GLA_HEADS)
    b = jnp.cumsum(to_chunks(log_a.astype(f32), GLA_HEADS), axis=3)
    b_last = b[:, :, :, -1:, :]
    causal = jnp.tril(jnp.ones((CHUNK, CHUNK), bool))
    q_in = q * jnp.exp(b)
    scores = jnp.einsum("bhnid,bhnjd->bhnij", q_in, k * jnp.exp(-b))
    scores = jnp.where(causal, scores, 0.0)
    o_intra = jnp.einsum("bhnij,bhnjv->bhniv", scores, v)
    k_state = k * jnp.exp(b_last - b)
    a_chunk = jnp.exp(b_last[:, :, :, 0, :])

    def step(S, xs):
        qc, kc, vc, ac = xs
        o = jnp.einsum("bhcd,bhdv->bhcv", qc, S)
        S = S * ac[..., None] + jnp.einsum("bhcd,bhcv->bhdv", kc, vc)
        return S, o

    bsz = q.shape[0]
    S0 = jnp.zeros((bsz, GLA_HEADS, GLA_DK, GLA_DV), f32)
    xs = (jnp.moveaxis(q_in, 2, 0), jnp.moveaxis(k_state, 2, 0),
          jnp.moveaxis(v, 2, 0), jnp.moveaxis(a_chunk, 2, 0))
    _, o_inter = lax.scan(step, S0, xs)
    return from_chunks(o_intra + jnp.moveaxis(o_inter, 0, 2))


def gated_delta_attention(q, k, v, g, beta):
    f32 = jnp.float32
    q = to_chunks(q.astype(f32), DN_HEADS)
    k = to_chunks(k.astype(f32), DN_HEADS)
    q = q * lax.rsqrt(jnp.sum(q * q, -1, keepdims=True) + EPS) * DN_DK ** -0.5
    k = k * lax.rsqrt(jnp.sum(k * k, -1, keepdims=True) + EPS)
    v = to_chunks(v.astype(f32), DN_HEADS)
    G = jnp.cumsum(scalar_chunks(g.astype(f32)), axis=-1)
    beta = scalar_chunks(beta.astype(f32))[..., None]
    causal = jnp.tril(jnp.ones((CHUNK, CHUNK), bool))
    strict = jnp.tril(jnp.ones((CHUNK, CHUNK), bool), k=-1)
    decay = jnp.exp(jnp.where(causal, G[..., :, None] - G[..., None, :], -jnp.inf))
    k_beta = k * beta
    kk = jnp.einsum("bhnid,bhnjd->bhnij", k_beta, k) * decay
    M = jnp.eye(CHUNK, dtype=f32) + jnp.where(strict, kk, 0.0)
    rhs = jnp.concatenate([v * beta, k_beta * jnp.exp(G)[..., None]], axis=-1)
    sol = lax.linalg.triangular_solve(M, rhs, left_side=True, lower=True, unit_diagonal=True)
    u, w = sol[..., :DN_DV], sol[..., DN_DV:]
    qk = jnp.einsum("bhnid,bhnjd->bhnij", q, k) * decay
    q_dec = q * jnp.exp(G)[..., None]
    G_last = G[..., -1:]
    k_state = k * jnp.exp(G_last - G)[..., None]
    g_chunk = jnp.exp(G_last[..., 0])

    def step(S, xs):
        qdc, qkc, uc, wc, kc, gc = xs
        v_new = uc - jnp.einsum("bhcd,bhdv->bhcv", wc, S)
        o = jnp.einsum("bhcd,bhdv->bhcv", qdc, S) + jnp.einsum("bhij,bhjv->bhiv", qkc, v_new)
        S = S * gc[..., None, None] + jnp.einsum("bhcd,bhcv->bhdv", kc, v_new)
        return S, o

    bsz = q.shape[0]
    S0 = jnp.zeros((bsz, DN_HEADS, DN_DK, DN_DV), f32)
    xs = tuple(jnp.moveaxis(t, 2, 0) for t in (q_dec, qk, u, w, k_state, g_chunk))
    _, o = lax.scan(step, S0, xs)
    return from_chunks(jnp.moveaxis(o, 0, 2))


def hybrid_mixer(h, w_in, w_gla_gate, b_gla_gate, conv_w, dn_a_log, dn_dt_bias,
                 gla_head_norm, dn_head_norm, w_out):
    proj = h @ w_in
    cuts = [int(c) for c in np.cumsum(IN_SIZES)[:-1]]
    (gq, gk, gv, gr, glr, dq, dk, dv, dgate, dbeta, da, merge_a, merge_b) = jnp.split(proj, cuts, axis=-1)
    log_a = jax.nn.log_sigmoid(glr @ w_gla_gate + b_gla_gate).astype(jnp.float32) / GLA_TAU
    o_a = gla_attention(gq, gk, gv, log_a)
    o_a = head_rms_norm(o_a, gla_head_norm, GLA_HEADS) * jax.nn.silu(gr)
    qkv = jax.nn.silu(causal_depthwise_conv(jnp.concatenate([dq, dk, dv], axis=-1), conv_w))
    dq, dk, dv = qkv[..., :DN_QK], qkv[..., DN_QK:2 * DN_QK], qkv[..., 2 * DN_QK:]
    g = -jnp.exp(dn_a_log.astype(jnp.float32)) * jax.nn.softplus((da + dn_dt_bias).astype(jnp.float32))
    beta = jax.nn.sigmoid(dbeta)
    o_b = gated_delta_attention(dq, dk, dv, g, beta)
    o_b = head_rms_norm(o_b, dn_head_norm, DN_HEADS) * jax.nn.silu(dgate)
    y = jax.nn.sigmoid(merge_a) * o_a + jax.nn.sigmoid(merge_b) * o_b
    return y.astype(h.dtype) @ w_out


def setup_inputs(seed: int = 0) -> dict:
    key = jax.random.key(seed)
    ks = jax.random.split(key, 24)
    L = DEPTH
    f32 = jnp.float32

    def dense(k, fan_in, shape):
        return jax.random.normal(k, shape, f32) * fan_in ** -0.5

    def gain(k, shape):
        return 1.0 + 0.02 * jax.random.normal(k, shape, f32)

    dt = jnp.exp(jax.random.uniform(ks[11], (L, DN_HEADS), f32, math.log(1e-3), math.log(1e-1)))
    return {
        "x": jax.random.normal(ks[0], (BATCH, SEQ, D_MODEL), f32),
        "ffn1_norm": gain(ks[1], (L, D_MODEL)),
        "ffn1_w_gate": dense(ks[2], D_MODEL, (L, D_MODEL, D_FF)),
        "ffn1_w_up": dense(ks[3], D_MODEL, (L, D_MODEL, D_FF)),
        "ffn1_w_down": dense(ks[4], D_FF, (L, D_FF, D_MODEL)),
        "mix_norm": gain(ks[5], (L, D_MODEL)),
        "w_in": dense(ks[6], D_MODEL, (L, D_MODEL, D_IN)),
        "w_gla_gate": dense(ks[7], GLA_RANK, (L, GLA_RANK, GLA_QK)),
        "b_gla_gate": 0.1 * jax.random.normal(ks[8], (L, GLA_QK), f32),
        "conv_w": dense(ks[9], CONV_K, (L, CONV_K, 2 * DN_QK + DN_V)),
        "dn_a_log": jnp.log(jax.random.uniform(ks[10], (L, DN_HEADS), f32, 1.0, 16.0)),
        "dn_dt_bias": dt + jnp.log(-jnp.expm1(-dt)),
        "gla_head_norm": gain(ks[12], (L, GLA_DV)),
        "dn_head_norm": gain(ks[13], (L, DN_DV)),
        "w_out": dense(ks[14], D_MODEL, (L, D_MODEL, D_MODEL)),
        "ffn2_norm": gain(ks[15], (L, D_MODEL)),
        "ffn2_w_gate": dense(ks[16], D_MODEL, (L, D_MODEL, D_FF)),
        "ffn2_w_up": dense(ks[17], D_MODEL, (L, D_MODEL, D_FF)),
        "ffn2_w_down": dense(ks[18], D_FF, (L, D_FF, D_MODEL)),
        "final_norm": gain(ks[19], (D_MODEL,)),
    }


def reference(x, ffn1_norm, ffn1_w_gate, ffn1_w_up, ffn1_w_down, mix_norm, w_in,
              w_gla_gate, b_gla_gate, conv_w, dn_a_log, dn_dt_bias, gla_head_norm,
              dn_head_norm, w_out, ffn2_norm, ffn2_w_gate, ffn2_w_up, ffn2_w_down,
              final_norm):
    h = x
    for layer in range(DEPTH):
        h = h + FFN_RES * swiglu(rms_norm(h, ffn1_norm[layer]), ffn1_w_gate[layer],
                                 ffn1_w_up[layer], ffn1_w_down[layer])
        h = h + hybrid_mixer(rms_norm(h, mix_norm[layer]), w_in[layer], w_gla_gate[layer],
                             b_gla_gate[layer], conv_w[layer], dn_a_log[layer], dn_dt_bias[layer],
                             gla_head_norm[layer], dn_head_norm[layer], w_out[layer])
        h = h + FFN_RES * swiglu(rms_norm(h, ffn2_norm[layer]), ffn2_w_gate[layer],
                                 ffn2_w_up[layer], ffn2_w_down[layer])
    return rms_norm(h, final_norm)
```

```python
import os
import numpy as np
import concourse.bass as bass
import concourse.mybir as mybir
from concourse.bass_utils import run_bass_kernel_spmd

F32 = mybir.dt.float32
BF16 = mybir.dt.bfloat16
AF = mybir.ActivationFunctionType
ALU = mybir.AluOpType
AX = mybir.AxisListType

D = 1024
KC = 8
DFF = 2816
FC = 22
D_IN = 9248
EPS = 1e-6
NEG = -30000.0
N_CORES = 8
SEQ = 8192
BATCH = 4

C_GQ, C_GK, C_GV, C_GR, C_GLR = 0, 512, 1024, 2048, 3072
C_DQ, C_DK, C_DV, C_DG, C_DB, C_DA = 3088, 4112, 5136, 6160, 7184, 7192
C_MA, C_MB = 7200, 8224

K_ID, K_UQ, K_US, K_CM, K_U64, K_USUF, K_NEGT, K_NEGS = range(8)
PK_N1, PK_NM, PK_N2, PK_CONV, PK_ALOG, PK_DTB, PK_END = 0, 8, 16, 24, 120, 128, 136
RW_B2, RW_GHN, RW_DHN, RW_FIN, RW_END = 0, 512, 768, 896, 1920


class Buf:
    __slots__ = ("name", "w", "r", "rd", "excl")

    def __init__(self, name, excl=False):
        self.name = name
        self.excl = excl
        self.w = None
        self.r = {}
        self.rd = []


class DSem:
    def __init__(self, h):
        self.h = h
        self.count = 0


class Op:
    __slots__ = ("eng", "fn", "deps", "sem", "value", "signal", "is_dma")


class Prog:
    def __init__(self, nc):
        self.nc = nc
        self.streams = {k: [] for k in ("pe", "act", "dve", "pool", "sp")}
        self.fence = []
        self.fence_id = 0
        self.seen = {k: 0 for k in self.streams}
        self.dma_recent = []
        self.out_dmas = []

    def op(self, eng, fn, reads=(), writes=(), dsem=None):
        o = Op()
        o.eng, o.fn, o.signal, o.is_dma = eng, fn, False, dsem is not None
        o.sem, o.value = None, 0
        deps = {}
        for b in reads:
            if b.w is not None:
                deps[id(b.w)] = b.w
            if b.excl:
                for k2, x in b.r.items():
                    if k2 != eng:
                        deps[id(x)] = x
        for b in writes:
            if b.w is not None:
                deps[id(b.w)] = b.w
            for x in b.r.values():
                deps[id(x)] = x
            for x in b.rd:
                deps[id(x)] = x
        if self.seen[eng] < self.fence_id:
            for x in self.fence:
                deps[id(x)] = x
            self.seen[eng] = self.fence_id
        o.deps = list(deps.values())
        for b in reads:
            if o.is_dma:
                b.rd.append(o)
            else:
                b.r[eng] = o
        for b in writes:
            b.w = o
            b.r = {}
            b.rd = []
        if o.is_dma:
            dsem.count += 16
            o.sem, o.value = dsem, dsem.count
            self.dma_recent.append(o)
        self.streams[eng].append(o)
        return o

    def barrier(self):
        f = []
        for k, s in self.streams.items():
            for o in reversed(s):
                if not o.is_dma:
                    f.append(o)
                    break
        f.extend(self.dma_recent)
        self.dma_recent = []
        self.fence = f
        self.fence_id += 1

    def dma(self, eng, out, in_, dsem, reads=(), writes=(), is_out=False):
        o = self.op(eng, lambda e: e.dma_start(out=out, in_=in_), reads, writes, dsem=dsem)
        if is_out:
            self.out_dmas.append(o)
        return o

    def mms(self, items, reads, writes):
        def fn(e):
            ins = None
            for (o, l, r, st, sp) in items:
                ins = e.matmul(o, lhsT=l, rhs=r, start=st, stop=sp)
            return ins
        return self.op("pe", fn, reads, writes)

    def mm(self, out, pairs, reads, writes):
        n = len(pairs)
        return self.mms([(out, l, r, i == 0, i == n - 1) for i, (l, r) in enumerate(pairs)],
                        reads, writes)

    def emit(self, eng_sems, block, all_dsems=()):
        for s in self.streams.values():
            for o in s:
                for d in o.deps:
                    d.signal = True
        for o in self.out_dmas:
            o.signal = True
        for k, s in self.streams.items():
            cnt = 0
            for o in s:
                if o.is_dma:
                    continue
                if o.signal:
                    cnt += 1
                    o.value = cnt
                    o.sem = eng_sems[k]

        final = list(self.out_dmas)

        def run(e, k):
            waited = {}
            for o in self.streams[k]:
                need = {}
                for d in o.deps:
                    if (not d.is_dma) and d.eng == "pe" and k == "pe":
                        continue
                    if SKIP_SAME and (not d.is_dma) and d.eng == k:
                        continue
                    key = id(d.sem)
                    if waited.get(key, 0) < d.value and need.get(key, (None, 0))[1] < d.value:
                        need[key] = (d.sem, d.value)
                for key, (sm, v) in need.items():
                    e.wait_ge(sm.h, v)
                    waited[key] = v
                ins = o.fn(e)
                if o.is_dma:
                    ins.then_inc(o.sem.h, 16)
                elif o.signal:
                    ins.then_inc(o.sem.h, 1)
            if k == "sp":
                for sm in all_dsems:
                    if sm.count > 0:
                        e.wait_ge(sm.h, sm.count)

        @block.sync
        def _(e):
            run(e, "sp")

        @block.tensor
        def _(e):
            run(e, "pe")

        @block.scalar
        def _(e):
            run(e, "act")

        @block.vector
        def _(e):
            run(e, "dve")

        @block.gpsimd
        def _(e):
            run(e, "pool")


class Arena:
    def __init__(self, nc, start=16640, end=229376):
        self.nc, self.start, self.end, self.cur, self.n = nc, start, end, start, 0

    def alloc(self, name, shape, dt):
        esz = 4 if dt == F32 else 2
        per = esz
        for s in shape[1:]:
            per *= s
        per = (per + 31) // 32 * 32
        assert self.cur + per <= self.end, (name, self.cur, per, self.end)
        self.n += 1
        t = self.nc.alloc_sbuf_tensor_at(f"{name}_{self.n}", list(shape), dt, offset=self.cur)
        self.last_off = self.cur
        self.cur += per
        return t

    def alias(self, name, shape, dt, off):
        self.n += 1
        return self.nc.alloc_sbuf_tensor_at(f"{name}_{self.n}", list(shape), dt, offset=off)

    def mark(self):
        return self.cur

    def reset(self, m):
        self.cur = m


DN_CUT = int(os.environ.get('DN_CUT', '0'))
EGV = int(os.environ.get('EGV', '0'))
SER = int(os.environ.get('SER', '0'))
INV_BF16 = int(os.environ.get('INV_BF16', '1'))
SKIP_SAME = int(os.environ.get('SKIP_SAME', '0'))


def build_program(NT, debug=False, upto=4):
    nc = bass.Bass("TRN2", target_bir_lowering=False)
    P = Prog(nc)
    A = Arena(nc)

    def din(name, shape):
        return nc.dram_tensor(name, list(shape), F32, kind="ExternalInput").ap()

    x_d = din("x", [NT, D])
    wg1_d, wu1_d, wd1_d = din("wg1", [D, DFF]), din("wu1", [D, DFF]), din("wd1", [DFF, D])
    wg2_d, wu2_d, wd2_d = din("wg2", [D, DFF]), din("wu2", [D, DFF]), din("wd2", [DFF, D])
    win_d, wout_d = din("w_in", [D, D_IN]), din("w_out", [D, D])
    w2_d = din("w2", [16, 512])
    consts_d = din("consts", [128, 8 * 128])
    pk_d = din("pk", [128, PK_END])
    rows_d = din("rows", [1, RW_END])
    out_d = nc.dram_tensor("out", [NT, D], F32, kind="ExternalOutput").ap()
    skind = "ExternalOutput" if debug else "Internal"
    h1_d = nc.dram_tensor("h1", [NT, D], F32, kind=skind).ap()
    ya_d = nc.dram_tensor("ya", [NT, D], F32, kind=skind).ap()
    y_d = nc.dram_tensor("yy", [NT, D], BF16, kind=skind).ap()
    gb_d = nc.dram_tensor("gbd", [NT, D], F32, kind="Internal").ap()

    psf = [nc.alloc_psum_tensor(f"psf{i}", [128, 512], F32) for i in range(7)]
    psb = [nc.alloc_psum_tensor("psb0", [128, 1024], BF16)]
    psf_b = [Buf(f"psf{i}", excl=True) for i in range(7)]
    psb_b = [Buf("psb0", excl=True)]

    class PsPool:
        def __init__(self, idxs):
            self.idxs, self.i = list(idxs), 0

        def next(self):
            k = self.idxs[self.i % len(self.idxs)]
            self.i += 1
            return psf[k], psf_b[k]

        def bank(self, j):
            k = self.idxs[j]
            return psf[k], psf_b[k]

    pool_box = [PsPool(range(7))]

    def next_psf():
        return pool_box[0].next()

    def next_psb():
        return psb[0], psb_b[0]

    sem_ctx = []

    def new_sem(name):
        cm = nc.semaphore(name)
        h = cm.__enter__()
        sem_ctx.append(cm)
        return h

    eng_sems = {k: DSem(new_sem("s_" + k)) for k in ("pe", "act", "dve", "pool")}
    dsems = {}

    def dsem(name):
        if name not in dsems:
            dsems[name] = DSem(new_sem("d_" + name))
        return dsems[name]

    consts = A.alloc("consts", [128, 8 * 128], F32)
    ident_b = A.alloc("identb", [128, 128], BF16)
    ones_b = A.alloc("onesb", [128, 128], BF16)
    pk = A.alloc("pk", [128, PK_END], F32)
    cb = Buf("consts")
    P.dma("sp", consts[:], consts_d, dsem("c0"), writes=[cb])
    P.dma("sp", pk[:], pk_d, dsem("c0"), writes=[cb])
    P.dma("pool", ident_b[:], consts_d[:, 0:128], dsem("c1"), writes=[cb])
    P.op("pool", lambda e: e.memset(ones_b[:], 1.0), writes=[cb])

    def cst(i):
        return consts[:, i * 128:(i + 1) * 128]

    persist_mark = A.mark()

    def rms_scale(src_tile, src_buf, TB, ss, rstd, sb, junk, jb, ncols=D, nsub=1):
        n = TB * nsub
        for j in range(n):
            tb, sub = divmod(j, nsub)
            w = ncols // nsub if nsub > 1 else ncols
            src = src_tile[:, tb, sub * w:(sub + 1) * w]
            P.op("act", lambda e, src=src, j=j, w=w: e.activation(
                out=junk[:, 0:w], in_=src, func=AF.Square, accum_out=ss[:, j:j + 1]),
                reads=[src_buf], writes=[jb, sb])
        wdt = float(ncols // nsub if nsub > 1 else ncols)
        P.op("act", lambda e: e.activation(out=rstd[:, 0:n], in_=ss[:, 0:n], func=AF.Ln, bias=EPS, scale=1.0 / wdt),
             reads=[sb], writes=[sb])
        P.op("act", lambda e: e.activation(out=rstd[:, 0:n], in_=rstd[:, 0:n], func=AF.Exp, scale=-0.5),
             reads=[sb], writes=[sb])

    def norm_transpose(xt, xt_b, TB, xn, xn_b, nT, nT_b, pk_off, ss, rstd, sb, junk, jb):
        TT = TB * 128
        rms_scale(xt, xt_b, TB, ss, rstd, sb, junk, jb)
        for tb in range(TB):
            P.op("act", lambda e, tb=tb: e.activation(out=xn[:, tb, :], in_=xt[:, tb, :], func=AF.Copy,
                                                      scale=rstd[:, tb:tb + 1]),
                 reads=[xt_b, sb], writes=[xn_b])
        for half in range(2):
            pst, pstb = next_psb()

            def fn(e, half=half, pst=pst):
                ins = None
                for kk in range(4):
                    k = half * 4 + kk
                    for tb in range(TB):
                        ins = e.transpose(out=pst[:, kk * TT + tb * 128: kk * TT + (tb + 1) * 128],
                                          in_=xn[:, tb, k * 128:(k + 1) * 128], identity=ident_b[:])
                return ins
            P.op("pe", fn, reads=[xn_b, cb], writes=[pstb])
            P.op("dve", lambda e, half=half, pst=pst: e.tensor_tensor(
                out=nT[:, half * 4:half * 4 + 4, :],
                in0=pst[:, 0:4 * TT].rearrange("p (k t) -> p k t", k=4),
                in1=pk[:, pk_off + half * 4: pk_off + half * 4 + 4].unsqueeze(2).to_broadcast([128, 4, TT]),
                op=ALU.mult), reads=[pstb, cb], writes=[nT_b])

    def load_w(dst, dst_b, src_d, c0, c1, name, rows_per=128):
        nk = src_d.shape[0] // 128
        for k in range(nk):
            P.dma("pool", dst[:, k, 0:c1 - c0], src_d[k * 128:(k + 1) * 128, c0:c1], dsem(name), writes=[dst_b])

    def ffn_stage(src_d, dst_d, wg_d, wu_d, wd_d, pk_off, final, with_mix):
        P.barrier()
        A.reset(persist_mark)
        TB = 2
        TT = 256
        ntiles = NT // TT
        Wg = A.alloc("Wg", [128, KC, DFF], BF16)
        Wu = A.alloc("Wu", [128, KC, DFF], BF16)
        Wd = A.alloc("Wd", [128, FC, D], BF16)
        wb = Buf("ffnw")
        wgu_b = [Buf("wgu0"), Buf("wgu1")]
        wd_b = Buf("wd")
        HC = 11 * 128
        if with_mix:
            Wo = A.alloc("Wo", [128, KC, D], BF16)
            load_w(Wo, wb, wout_d, 0, D, "w0")
        for hf in range(2):
            for (dst, src) in ((Wg, wg_d), (Wu, wu_d)):
                for k in range(KC):
                    P.dma("pool", dst[:, k, hf * HC:(hf + 1) * HC], src[k * 128:(k + 1) * 128, hf * HC:(hf + 1) * HC],
                          dsem(f"wg{hf}"), writes=[wgu_b[hf]])
        load_w(Wd, wd_b, wd_d, 0, D, "w1")
        if with_mix:
            yt1 = A.alloc("yt", [128, TB, D], BF16)
            yt = [yt1, yt1]
            yb1 = Buf("yt")
            yt_b = [yb1, yb1]
            yT = A.alloc("yT", [128, KC, TT], BF16)
            yT_b = Buf("yT")
        if final:
            fin = A.alloc("fin", [128, D], F32)
            P.dma("sp", fin[:], rows_d[:, RW_FIN:RW_FIN + D].partition_broadcast(128), dsem("c0"), writes=[wb])
        xt = [A.alloc("xt", [128, TB, D], F32) for _ in range(2)]
        xt_b = [Buf("xt") for _ in range(2)]
        xn = A.alloc("xn", [128, TB, D], BF16)
        xn_b = Buf("xn")
        nT = [A.alloc("nT", [128, KC, TT], BF16) for _ in range(2)]
        nT_b = [Buf("nT") for _ in range(2)]
        hT = A.alloc("hT", [128, FC, TT], BF16)
        hT_b = Buf("hT")
        sil = [A.alloc("sil", [128, TT], F32) for _ in range(2)]
        sil_b = [Buf("sil") for _ in range(2)]
        junk = xn[:, 0, :]
        jb = xn_b
        ss = A.alloc("ss", [128, 4], F32)
        rstd = A.alloc("rstd", [128, 4], F32)
        sb = Buf("ss")

        def prologue(t):
            s = t % 2
            r0 = t * TT
            P.dma("sp", xt[s][:], src_d[r0:r0 + TT, :].rearrange("(tb p) d -> p tb d", p=128),
                  dsem(f"ld{s}"), writes=[xt_b[s]])
            if with_mix:
                P.dma("sp", yt[s][:], y_d[r0:r0 + TT, :].rearrange("(tb p) d -> p tb d", p=128),
                      dsem("ldy"), writes=[yt_b[s]])
                for half in range(2):
                    pst, pstb = next_psb()

                    def fn(e, half=half, pst=pst, s=s):
                        ins = None
                        for kk in range(4):
                            k = half * 4 + kk
                            for tb in range(TB):
                                ins = e.transpose(out=pst[:, kk * TT + tb * 128: kk * TT + (tb + 1) * 128],
                                                  in_=yt[s][:, tb, k * 128:(k + 1) * 128], identity=ident_b[:])
                        return ins
                    P.op("pe", fn, reads=[yt_b[s], cb], writes=[pstb])
                    P.op("act", lambda e, half=half, pst=pst: e.activation(
                        out=yT[:, half * 4:half * 4 + 4, :],
                        in_=pst[:, 0:4 * TT].rearrange("p (k t) -> p k t", k=4), func=AF.Copy),
                        reads=[pstb], writes=[yT_b])
                for tb in range(TB):
                    for ch in range(2):
                        ps, psb_ = next_psf()
                        P.mm(ps[:, :], [(yT[:, k, tb * 128:(tb + 1) * 128], Wo[:, k, ch * 512:(ch + 1) * 512])
                                        for k in range(KC)], reads=[yT_b, wb], writes=[psb_])
                        P.op("dve", lambda e, ps=ps, tb=tb, ch=ch, s=s: e.tensor_tensor(
                            out=xt[s][:, tb, ch * 512:(ch + 1) * 512], in0=ps[:, :],
                            in1=xt[s][:, tb, ch * 512:(ch + 1) * 512], op=ALU.add),
                            reads=[psb_, xt_b[s]], writes=[xt_b[s]])
            norm_transpose(xt[s], xt_b[s], TB, xn, xn_b, nT[s], nT_b[s], pk_off, ss, rstd, sb, junk, jb)

        prologue(0)
        for t in range(ntiles):
            s = t % 2
            r0 = t * TT
            for f in range(FC):
                ps, psb_ = next_psf()
                items = []
                for k in range(KC):
                    items.append((ps[:, 0:TT], Wg[:, k, f * 128:(f + 1) * 128], nT[s][:, k, :], k == 0, k == KC - 1))
                for k in range(KC):
                    items.append((ps[:, TT:2 * TT], Wu[:, k, f * 128:(f + 1) * 128], nT[s][:, k, :], k == 0, k == KC - 1))
                P.mms(items, reads=[wgu_b[f // 11], nT_b[s]], writes=[psb_])
                s2 = f % 2
                P.op("act", lambda e, ps=ps, s2=s2: e.activation(out=sil[s2][:], in_=ps[:, 0:TT], func=AF.Silu),
                     reads=[psb_], writes=[sil_b[s2]])
                P.op("dve", lambda e, ps=ps, s2=s2, f=f: e.tensor_tensor(
                    out=hT[:, f, :], in0=sil[s2][:], in1=ps[:, TT:2 * TT], op=ALU.mult),
                    reads=[psb_, sil_b[s2]], writes=[hT_b])
            if t + 1 < ntiles:
                prologue(t + 1)
            for tb in range(TB):
                for ch in range(2):
                    ps, psb_ = next_psf()
                    P.mm(ps[:, :], [(hT[:, f, tb * 128:(tb + 1) * 128], Wd[:, f, ch * 512:(ch + 1) * 512])
                                    for f in range(FC)], reads=[hT_b, wd_b], writes=[psb_])
                    P.op("dve", lambda e, ps=ps, tb=tb, ch=ch, s=s: e.scalar_tensor_tensor(
                        out=xt[s][:, tb, ch * 512:(ch + 1) * 512], in0=ps[:, :], scalar=0.5,
                        in1=xt[s][:, tb, ch * 512:(ch + 1) * 512], op0=ALU.mult, op1=ALU.add),
                        reads=[psb_, xt_b[s]], writes=[xt_b[s]])
            if final:
                rms_scale(xt[s], xt_b[s], TB, ss, rstd, sb, junk, jb)
                for tb in range(TB):
                    P.op("dve", lambda e, tb=tb, s=s: e.scalar_tensor_tensor(
                        out=xt[s][:, tb, :], in0=xt[s][:, tb, :], scalar=rstd[:, tb:tb + 1],
                        in1=fin[:], op0=ALU.mult, op1=ALU.mult),
                        reads=[xt_b[s], sb, wb], writes=[xt_b[s]])
            P.dma("sp", dst_d[r0:r0 + TT, :].rearrange("(tb p) d -> p tb d", p=128), xt[s][:],
                  dsem(f"st{s}"), reads=[xt_b[s]], is_out=final)

    def gla_stage():
        P.barrier()
        A.reset(persist_mark)
        TB, TT = 2, 256
        ntiles = NT // TT
        NW = C_GLR + 16
        Wa = A.alloc("Wa", [128, KC, NW], BF16)
        Wm = A.alloc("Wm", [128, KC, D], BF16)
        Wdg = A.alloc("Wdg", [128, KC, D], BF16)
        Wmb = A.alloc("Wmb", [128, KC, D], BF16)
        wb = Buf("glaw")
        wb2 = Buf("glaw2")
        load_w(Wa, wb, win_d, 0, NW, "w0")
        load_w(Wm, wb2, win_d, C_MA, C_MA + D, "w1")
        load_w(Wdg, wb2, win_d, C_DG, C_DG + D, "w1")
        load_w(Wmb, wb2, win_d, C_MB, C_MB + D, "w1")
        w2 = A.alloc("w2", [16, 512], F32)
        b2 = A.alloc("b2", [1, 512], F32)
        ones_r = A.alloc("onesr", [1, 128], F32)
        ghn = A.alloc("ghn", [128, 256], F32)
        P.dma("sp", w2[:], w2_d, dsem("c0"), writes=[wb])
        P.dma("sp", b2[:], rows_d[:, RW_B2:RW_B2 + 512], dsem("c0"), writes=[wb])
        P.dma("sp", ghn[:], rows_d[:, RW_GHN:RW_GHN + 256].partition_broadcast(128), dsem("c0"), writes=[wb])
        P.op("pool", lambda e: e.memset(ones_r[:], 1.0), writes=[wb])
        S = A.alloc("S", [128, 4, 256], F32)
        Sb = A.alloc("Sb", [128, 4, 256], BF16)
        S_b, Sb_b = Buf("S"), Buf("Sb")
        P.op("pool", lambda e: e.memset(S[:], 0.0), writes=[S_b])
        P.op("pool", lambda e: e.memset(Sb[:], 0.0), writes=[Sb_b])

        ht, ht_b = A.alloc("ht", [128, TB, D], F32), Buf("ht")
        xn, xn_b = A.alloc("xn", [128, TB, D], BF16), Buf("xn")
        nT, nT_b = A.alloc("nT", [128, KC, TT], BF16), Buf("nT")
        junk, jb = xn[:, 0, :], xn_b
        ss = A.alloc("ss", [128, 4], F32)
        rstd = A.alloc("rstd", [128, 4], F32)
        sb = Buf("ss")
        ss2 = A.alloc("ss2", [128, 4], F32)
        ss2_b = Buf("ss2")
        junk2, j2b = A.alloc("junk2", [128, 256], BF16), Buf("junk2")
        sg, sg_b = A.alloc("sg", [128, 512], F32), Buf("sg")
        gst, gst_b = A.alloc("gst", [128, D], F32), Buf("gst")
        glrT, glr_b = A.alloc("glrT", [16, TT], F32), Buf("glrT")
        zt, zt_b = sg, sg_b

        def two(name, shape, dt):
            return [A.alloc(name, shape, dt) for _ in range(2)], [Buf(name + "0"), Buf(name + "1")]

        qT, qT_b = two("qT", [128, 4, TT], F32)
        kT, kT_b = two("kT", [128, 4, TT], F32)
        ktm, ktm_b = two("ktm", [128, TB, 512], F32)
        vbf, vbf_b = two("vbf", [128, TB, D], BF16)
        ga, ga_b = two("ga", [128, TB, D], F32)
        Lt, Lt_b = two("Lt", [128, TB, 512], F32)
        E1, E1_b = A.alloc("E1", [128, 4, 128], F32), Buf("E1")
        E2, E2_b = A.alloc("E2", [128, 4, 128], F32), Buf("E2")
        E3, E3_b = A.alloc("E3", [128, 512], F32), Buf("E3")
        qin, qin_b = A.alloc("qin", [128, 4, 128], BF16), Buf("qin")
        kout, kout_b = A.alloc("kout", [128, 4, 128], BF16), Buf("kout")
        kst, kst_b = A.alloc("kst", [128, 512], BF16), Buf("kst")
        scT, scT_b = A.alloc("scT", [128, 4, 128], BF16), Buf("scT")
        osb, osb_b = A.alloc("osb", [128, D], F32), Buf("osb")

        def act(out, in_, func, reads, writes, **kw):
            P.op("act", lambda e: e.activation(out=out, in_=in_, func=func, **kw), reads, writes)

        def tt(eng, out, in0, in1, op, reads, writes):
            P.op(eng, lambda e: e.tensor_tensor(out=out, in0=in0, in1=in1, op=op), reads, writes)

        def stt(out, in0, scalar, in1, op0, op1, reads, writes):
            P.op("dve", lambda e: e.scalar_tensor_tensor(out=out, in0=in0, scalar=scalar, in1=in1, op0=op0, op1=op1), reads, writes)

        cpool = PsPool([0, 1, 2, 3])
        fpool = PsPool([4, 5, 6])

        def front(t):
            next_psf = fpool.next
            s = t % 2
            r0 = t * TT
            P.dma("sp", ht[:], h1_d[r0:r0 + TT, :].rearrange("(tb p) d -> p tb d", p=128), dsem("ld0"), writes=[ht_b])
            yield
            norm_transpose(ht, ht_b, TB, xn, xn_b, nT, nT_b, PK_NM, ss, rstd, sb, junk, jb)
            yield
            for (dst, dst_b, c0) in ((qT[s], qT_b[s], C_GQ), (kT[s], kT_b[s], C_GK)):
                for j in range(2):
                    ps, pb = next_psf()
                    items = []
                    for hh in range(2):
                        h = j * 2 + hh
                        for k in range(KC):
                            items.append((ps[:, hh * TT:(hh + 1) * TT], Wa[:, k, c0 + h * 128:c0 + (h + 1) * 128],
                                          nT[:, k, :], k == 0, k == KC - 1))
                    P.mms(items, reads=[wb, nT_b], writes=[pb])
                    act(dst[:, j * 2:j * 2 + 2, :], ps[:, :].rearrange("p (h t) -> p h t", h=2), AF.Copy, [pb], [dst_b])
                    yield
            ps, pb = next_psf()
            P.mm(ps[0:16, 0:TT], [(Wa[:, k, C_GLR:C_GLR + 16], nT[:, k, :]) for k in range(KC)], reads=[wb, nT_b], writes=[pb])
            act(glrT[:], ps[0:16, 0:TT], AF.Copy, [pb], [glr_b])
            yield
            for tb in range(TB):
                tok = slice(tb * 128, (tb + 1) * 128)
                ps, pb = next_psf()
                P.mms([(ps[:, :], glrT[:, tok], w2[:], True, False),
                       (ps[:, :], ones_r[:], b2[:], False, True)], reads=[glr_b, wb], writes=[pb])
                act(zt[:], ps[:, :], AF.Exp, [pb], [zt_b], scale=-1.0)
                act(Lt[s][:, tb, :], zt[:], AF.Ln, [zt_b], [Lt_b[s]], bias=1.0)
                yield
                ps, pb = next_psf()
                P.mm(ps[:, :], [(nT[:, k, tok], Wa[:, k, C_GK:C_GK + 512]) for k in range(KC)], reads=[wb, nT_b], writes=[pb])
                act(ktm[s][:, tb, :], ps[:, :], AF.Copy, [pb], [ktm_b[s]])
                yield
                for ch in range(2):
                    cs = slice(ch * 512, (ch + 1) * 512)
                    ps, pb = next_psf()
                    P.mm(ps[:, :], [(nT[:, k, tok], Wa[:, k, C_GV + ch * 512:C_GV + (ch + 1) * 512]) for k in range(KC)],
                         reads=[wb, nT_b], writes=[pb])
                    act(vbf[s][:, tb, cs], ps[:, :], AF.Copy, [pb], [vbf_b[s]])
                    yield
                for ch in range(2):
                    cs = slice(ch * 512, (ch + 1) * 512)
                    for (Wg_, cg, dst, dst_b) in ((Wa, C_GR + ch * 512, ga[s][:, tb, cs], ga_b[s]), (Wdg, ch * 512, gst[:, cs], gst_b)):
                        pg_, pgb_ = next_psf()
                        P.mm(pg_[:, :], [(nT[:, k, tok], Wg_[:, k, cg:cg + 512]) for k in range(KC)], reads=[wb, wb2, nT_b], writes=[pgb_])
                        act(dst, pg_[:, :], AF.Silu, [pgb_], [dst_b])
                yield
                for ch in range(2):
                    cs = slice(ch * 512, (ch + 1) * 512)
                    for (Wm_, dst, dst_b) in ((Wm, ga[s][:, tb, cs], ga_b[s]), (Wmb, gst[:, cs], gst_b)):
                        pm_, pmb_ = next_psf()
                        P.mm(pm_[:, :], [(nT[:, k, tok], Wm_[:, k, cs]) for k in range(KC)], reads=[wb, wb2, nT_b], writes=[pmb_])
                        act(sg[:], pm_[:, :], AF.Sigmoid, [pmb_], [sg_b])
                        tt("dve", dst, dst, sg[:], ALU.mult, [sg_b, dst_b], [dst_b])
                yield
                P.dma("sp", gb_d[r0 + tb * 128:r0 + (tb + 1) * 128, :], gst[:], dsem("stg"), reads=[gst_b])
                yield

        def chunks(t):
            s = t % 2
            r0 = t * TT
            for tb in range(TB):
                tok = slice(tb * 128, (tb + 1) * 128)
                ps, pb = cpool.bank(0)
                P.mms([(ps[:, h * 128:(h + 1) * 128], Lt[s][:, tb, h * 128:(h + 1) * 128], cst(K_UQ), True, True)
                       for h in range(4)], reads=[Lt_b[s], cb], writes=[pb])
                pd, pdb = cpool.bank(1)
                P.mm(pd[:, :], [(cst(K_US), Lt[s][:, tb, :])], reads=[Lt_b[s], cb], writes=[pdb])
                yield
                act(E1[:], ps[:, :].rearrange("p (h t) -> p h t", h=4), AF.Exp, [pb], [E1_b])
                act(E2[:], ps[:, :].rearrange("p (h t) -> p h t", h=4), AF.Exp, [pb], [E2_b], scale=-1.0)
                act(E3[:], pd[:, :], AF.Exp, [pdb], [E3_b])
                yield
                stt(qin[:], qT[s][:, :, tok], 128.0 ** -0.5, E1[:], ALU.mult, ALU.mult, [qT_b[s], E1_b], [qin_b])
                tt("dve", kout[:], kT[s][:, :, tok], E2[:], ALU.mult, [kT_b[s], E2_b], [kout_b])
                tt("pool", kst[:], ktm[s][:, tb, :], E3[:], ALU.mult, [ktm_b[s], E3_b], [kst_b])
                yield
                ps, pb = cpool.bank(0)
                P.mms([(ps[:, h * 128:(h + 1) * 128], kout[:, h, :], qin[:, h, :], True, True) for h in range(4)],
                      reads=[kout_b, qin_b], writes=[pb])
                yield
                tt("dve", scT[:], ps[:, :].rearrange("p (h t) -> p h t", h=4),
                   cst(K_CM).unsqueeze(1).to_broadcast([128, 4, 128]), ALU.mult, [pb, cb], [scT_b])
                yield
                po = [cpool.bank(2), cpool.bank(3)]
                for j in range(2):
                    items = []
                    for hh in range(2):
                        h = j * 2 + hh
                        items.append((po[j][0][:, hh * 256:(hh + 1) * 256], qin[:, h, :], Sb[:, h, :], True, False))
                        items.append((po[j][0][:, hh * 256:(hh + 1) * 256], scT[:, h, :], vbf[s][:, tb, h * 256:(h + 1) * 256], False, True))
                    P.mms(items, reads=[qin_b, Sb_b, scT_b, vbf_b[s]], writes=[po[j][1]])
                pu = [cpool.bank(0), cpool.bank(1)]
                for j in range(2):
                    items = []
                    for hh in range(2):
                        h = j * 2 + hh
                        items.append((pu[j][0][:, hh * 256:(hh + 1) * 256], kst[:, h * 128:(h + 1) * 128],
                                      vbf[s][:, tb, h * 256:(h + 1) * 256], True, True))
                    P.mms(items, reads=[kst_b, vbf_b[s]], writes=[pu[j][1]])
                yield
                for h in range(4):
                    j, hh = divmod(h, 2)
                    stt(S[:, h, :], S[:, h, :], E1[:, h, 127:128], pu[j][0][:, hh * 256:(hh + 1) * 256], ALU.mult, ALU.add,
                        [S_b, E1_b, pu[j][1]], [S_b])
                yield
                act(Sb[:], S[:], AF.Copy, [S_b], [Sb_b])
                yield
                for h in range(4):
                    j, hh = divmod(h, 2)
                    act(junk2[:], po[j][0][:, hh * 256:(hh + 1) * 256], AF.Square, [po[j][1]], [j2b, ss2_b],
                        accum_out=ss2[:, h:h + 1])
                yield
                act(ss2[:], ss2[:], AF.Ln, [ss2_b], [ss2_b], bias=EPS, scale=1.0 / 256.0)
                act(ss2[:], ss2[:], AF.Exp, [ss2_b], [ss2_b], scale=-0.5)
                yield
                for h in range(4):
                    j, hh = divmod(h, 2)
                    stt(osb[:, h * 256:(h + 1) * 256], po[j][0][:, hh * 256:(hh + 1) * 256], ss2[:, h:h + 1], ghn[:],
                        ALU.mult, ALU.mult, [po[j][1], ss2_b, wb], [osb_b])
                yield
                tt("pool", osb[:], osb[:], ga[s][:, tb, :], ALU.mult, [osb_b, ga_b[s]], [osb_b])
                P.dma("sp", ya_d[r0 + tb * 128:r0 + (tb + 1) * 128, :], osb[:], dsem("st0"), reads=[osb_b])
                yield

        def run_rr(gens):
            gens = list(gens)
            while gens:
                for g in list(gens):
                    try:
                        next(g)
                    except StopIteration:
                        gens.remove(g)

        run_rr([front(0)])
        for t in range(ntiles):
            gens = [chunks(t)]
            if t + 1 < ntiles:
                gens.append(front(t + 1))
            run_rr(gens)

    def dn_stage():
        P.barrier()
        A.reset(persist_mark)
        TT = 128
        ntiles = NT // TT
        O_BA = 3072
        Wb = A.alloc("Wb", [128, KC, 3088], BF16)
        wb = Buf("dnw")
        load_w(Wb, wb, win_d, C_DQ, C_DQ + 3072, "w0")
        for k in range(KC):
            P.dma("pool", Wb[:, k, 3072:3088], win_d[k * 128:(k + 1) * 128, C_DB:C_DB + 16], dsem("w0"), writes=[wb])
        dhn = A.alloc("dhn", [128, 128], F32)
        P.dma("sp", dhn[:], rows_d[:, RW_DHN:RW_DHN + 128].partition_broadcast(128), dsem("c0"), writes=[wb])
        nega = A.alloc("nega", [128, 8], F32)
        P.op("act", lambda e: e.activation(out=nega[:], in_=pk[:, PK_ALOG:PK_ALOG + 8], func=AF.Exp), reads=[cb], writes=[wb])
        P.op("dve", lambda e: e.tensor_scalar(out=nega[:], in0=nega[:], scalar1=-1.0, scalar2=None, op0=ALU.mult),
             reads=[wb], writes=[wb])

        def hb(name):
            return [Buf(name + "0"), Buf(name + "1")]

        def t3(name, dt=F32):
            return A.alloc(name, [128, 8, 128], dt), hb(name)

        S, S_b = t3("S")
        Sb, Sb_b = t3("Sb", BF16)
        for j in range(2):
            P.op("pool", lambda e, j=j: e.memset(S[:, j * 4:j * 4 + 4, :], 0.0), writes=[S_b[j]])
            P.op("pool", lambda e, j=j: e.memset(Sb[:, j * 4:j * 4 + 4, :], 0.0), writes=[Sb_b[j]])
        xc = A.alloc("xc", [128, 24, TT + 3], F32)
        xc_b = [Buf(f"xc{g}") for g in range(6)]
        P.op("pool", lambda e: e.memset(xc[:], 0.0), writes=xc_b)
        cv = A.alloc("cv", [128, 24, TT], F32)
        cvc_b = [Buf(f"cv{c}") for c in range(16)]
        cvg_b = [Buf("cvg4"), Buf("cvg5")]
        ctmp = [A.alloc("ctmp", [128, 4, TT], F32) for _ in range(2)]
        ctmp_b = [Buf("ctmp0"), Buf("ctmp1")]

        ht, ht_b = A.alloc("ht", [128, 1, D], F32), Buf("ht")
        yat, yat_b = A.alloc("yat", [128, D], F32), Buf("yat")
        gbt, gbt_b = A.alloc("gbt", [128, D], F32), Buf("gbt")
        xn, xn_b = A.alloc("xn", [128, 1, D], BF16), Buf("xn")
        nT, nT_b = A.alloc("nT", [128, KC, TT], BF16), Buf("nT")
        junk, jb = xn[:, 0, :], xn_b
        ss = A.alloc("ss", [128, 16], F32)
        rstd = A.alloc("rstd", [128, 16], F32)
        sb = Buf("ss")
        ss2 = A.alloc("ss2", [128, 8], F32)
        ss2_b = hb("ss2")
        sc2 = [A.alloc("sc", [128, 64], F32) for _ in range(2)]
        sc2_b = [Buf("sc0"), Buf("sc1")]
        qh, qh_b = t3("qh", BF16)
        kh, kh_b = t3("kh", BF16)
        vT, vT_b = t3("vTb", BF16)
        ktm, ktm_b = t3("ktm", BF16)
        vtm, vtm_b = t3("vtm", BF16)
        MG, MG_b = t3("MG")
        MG2, MG2_b = t3("MG2")
        EG, EG_b = t3("EG")
        Et, Et_b = t3("Et")
        En, En_b = t3("En")
        IDT = BF16 if INV_BF16 else F32
        Bm, Bm_b = t3("Bm", IDT)
        Bt, Bt_b = t3("Bt", IDT)
        Bm2, Bm2_b = t3("Bm2", IDT)
        Bt2, Bt2_b = t3("Bt2", IDT)
        RT, RT_b = t3("RT", IDT)
        rn = A.alloc("rn", [128, 16, 128], F32)
        rnq_bs, rnk_bs = hb("rnq"), hb("rnk")
        sq = A.alloc("sq", [128, 16, 128], BF16)
        sq_b = hb("sq")
        if INV_BF16:
            RTb, RTb_b = RT, RT_b
        else:
            RTb, RTb_b = t3("RTb", BF16)
        qkT, qkT_b = t3("qkT", BF16)
        vbt, vbt_b = t3("vbt", BF16)
        kbg, kbg_b = t3("kbg", BF16)
        kst, kst_b = t3("kst", BF16)
        qd, qd_b = t3("qd", BF16)
        wT, wT_b = t3("wT", BF16)
        us, us_b = t3("us")
        gbm = A.alias("gbm", [128, 8, 128], F32, A.last_off)
        gbm_b = us_b
        vn, vn_b = t3("vn", BF16)
        osb, osb_b = t3("osb")
        DG, DG_b = t3("DG")
        yo, yo_b = A.alloc("yo", [128, D], BF16), hb("yo")

        def act(out, in_, func, reads, writes, **kw):
            P.op("act", lambda e: e.activation(out=out, in_=in_, func=func, **kw), reads, writes)

        def tt(eng, out, in0, in1, op, reads, writes):
            P.op(eng, lambda e: e.tensor_tensor(out=out, in0=in0, in1=in1, op=op), reads, writes)

        def stt(out, in0, scalar, in1, op0, op1, reads, writes):
            P.op("dve", lambda e: e.scalar_tensor_tensor(out=out, in0=in0, scalar=scalar, in1=in1, op0=op0, op1=op1), reads, writes)

        def ts(eng, out, in0, s1, s2, op0, op1, reads, writes):
            if s2 is None:
                P.op(eng, lambda e: e.tensor_scalar(out=out, in0=in0, scalar1=s1, scalar2=None, op0=op0), reads, writes)
            else:
                P.op(eng, lambda e: e.tensor_scalar(out=out, in0=in0, scalar1=s1, scalar2=s2, op0=op0, op1=op1), reads, writes)

        def v3(ps):
            return ps[:, :].rearrange("p (h t) -> p h t", h=4)

        def bc4(sc, col0, j):
            return sc[:, col0 + j * 4:col0 + j * 4 + 4].unsqueeze(2).to_broadcast([128, 4, 128])

        def cbc(i):
            return cst(i).unsqueeze(1).to_broadcast([128, 4, 128])

        def cw(g, tap):
            return pk[:, PK_CONV + g * 16:PK_CONV + g * 16 + 16].rearrange("p (c j) -> p c j", j=4)[:, :, tap:tap + 1].to_broadcast([128, 4, TT])

        hpool = [PsPool([0, 1, 2]), PsPool([3, 4, 5])]
        fpool = PsPool([6])

        def front(t):
            next_psf = fpool.next
            r0 = t * TT
            P.dma("sp", ht[:, 0, :], h1_d[r0:r0 + TT, :], dsem("ld0"), writes=[ht_b])
            yield
            norm_transpose(ht, ht_b, 1, xn, xn_b, nT, nT_b, PK_NM, ss, rstd, sb, junk, jb)
            yield
            for g in range(6):
                ps, pb = next_psf()
                items = []
                for cc in range(4):
                    c = g * 4 + cc
                    for k in range(KC):
                        items.append((ps[:, cc * 128:(cc + 1) * 128], Wb[:, k, c * 128:(c + 1) * 128], nT[:, k, :],
                                      k == 0, k == KC - 1))
                P.mms(items, reads=[wb, nT_b], writes=[pb])
                act(xc[:, g * 4:g * 4 + 4, 3:3 + TT], v3(ps), AF.Copy, [pb], [xc_b[g]])
                yield
            for tap in range(4):
                for c in range(16):
                    g = c // 4
                    wcol = pk[:, PK_CONV + c * 4 + tap:PK_CONV + c * 4 + tap + 1]
                    if tap == 0:
                        ts("dve", cv[:, c, :], xc[:, c, 0:TT], wcol, None, ALU.mult, None, [xc_b[g], cb], [cvc_b[c]])
                    else:
                        stt(cv[:, c, :], xc[:, c, tap:tap + TT], wcol, cv[:, c, :], ALU.mult, ALU.add,
                            [xc_b[g], cb, cvc_b[c]], [cvc_b[c]])
                    if c % 4 == 3:
                        yield
            for gi, g in enumerate((4, 5)):
                gs = slice(g * 4, g * 4 + 4)
                tt("pool", cv[:, gs, :], xc[:, gs, 0:TT], cw(g, 0), ALU.mult, [xc_b[g], cb], [cvg_b[gi]])
            yield
            for tap in range(1, 4):
                for gi, g in enumerate((4, 5)):
                    gs = slice(g * 4, g * 4 + 4)
                    tt("pool", ctmp[gi][:], xc[:, gs, tap:tap + TT], cw(g, tap), ALU.mult, [xc_b[g], cb], [ctmp_b[gi]])
                    tt("pool", cv[:, gs, :], cv[:, gs, :], ctmp[gi][:], ALU.add, [ctmp_b[gi], cvg_b[gi]], [cvg_b[gi]])
                yield
            for g in range(6):
                gs = slice(g * 4, g * 4 + 4)
                cvb = cvc_b[g * 4:g * 4 + 4] if g < 4 else [cvg_b[g - 4]]
                P.op("pool", lambda e, gs=gs: e.tensor_copy(out=xc[:, gs, 0:3], in_=xc[:, gs, TT:TT + 3]),
                     reads=[xc_b[g]] + cvb, writes=[xc_b[g]])
            yield
            allcv = cvc_b + cvg_b
            act(cv[:], cv[:], AF.Silu, allcv, allcv)
            yield
            yield from scal(t)

        def scal(t):
            next_psf = fpool.next
            sc, sc_b = sc2[t % 2], sc2_b[t % 2]
            ps, pb = next_psf()
            P.mm(ps[:, 0:16], [(nT[:, k, :], Wb[:, k, O_BA:O_BA + 16]) for k in range(KC)], reads=[wb, nT_b], writes=[pb])
            act(sc[:, 0:8], ps[:, 0:8], AF.Exp, [pb], [sc_b], scale=-1.0)
            ts("dve", sc[:, 0:8], sc[:, 0:8], 1.0, None, ALU.add, None, [sc_b], [sc_b])
            P.op("dve", lambda e: e.reciprocal(out=sc[:, 0:8], in_=sc[:, 0:8]), reads=[sc_b], writes=[sc_b])
            yield
            tt("dve", sc[:, 48:56], ps[:, 8:16], pk[:, PK_DTB:PK_DTB + 8], ALU.add, [pb, cb, sc_b], [sc_b])
            yield
            act(sc[:, 48:56], sc[:, 48:56], AF.Exp, [sc_b], [sc_b])
            act(sc[:, 48:56], sc[:, 48:56], AF.Ln, [sc_b], [sc_b], bias=1.0)
            yield
            tt("dve", sc[:, 8:16], sc[:, 48:56], nega[:], ALU.mult, [sc_b, wb], [sc_b])
            ts("dve", sc[:, 24:32], sc[:, 0:8], -1.0, None, ALU.mult, None, [sc_b], [sc_b])
            yield
            ps, pb = next_psf()
            P.mms([(ps[:, 0:8], cst(K_U64), sc[:, 8:16], True, True),
                   (ps[:, 8:16], cst(K_USUF), sc[:, 8:16], True, True)], reads=[sc_b, cb], writes=[pb])
            yield
            act(sc[:, 16:24], ps[:, 0:8], AF.Copy, [pb, sc_b], [sc_b])
            act(sc[:, 56:64], ps[:, 0:8], AF.Exp, [pb, sc_b], [sc_b])
            act(sc[:, 40:48], ps[:, 8:16], AF.Exp, [pb, sc_b], [sc_b])
            yield
            tt("dve", sc[:, 32:40], sc[:, 56:64], sc[:, 0:8], ALU.mult, [sc_b], [sc_b])
            yield

        def half(t, j):
            next_psf = hpool[j].next
            hs = slice(j * 4, j * 4 + 4)
            r0 = t * TT
            sc, sc_b = sc2[t % 2], sc2_b[t % 2]
            cvq = cvc_b[j * 4:j * 4 + 4]
            cvk = cvc_b[8 + j * 4:8 + j * 4 + 4]
            cvv = [cvg_b[j]]
            qs, ks, vs = slice(j * 4, j * 4 + 4), slice(8 + j * 4, 12 + j * 4), slice(16 + j * 4, 20 + j * 4)
            rnq_b, rnk_b = rnq_bs[j], rnk_bs[j]
            sqq, sqk = slice(j * 8, j * 8 + 4), slice(j * 8 + 4, j * 8 + 8)
            act(sq[:, sqq, :], cv[:, qs, :], AF.Square, cvq, [sq_b[j]])
            act(sq[:, sqk, :], cv[:, ks, :], AF.Square, cvk, [sq_b[j]])
            P.op("pool", lambda e: e.tensor_copy(out=vT[:, hs, :], in_=cv[:, vs, :]), reads=cvv, writes=[vT_b[j]])
            yield
            for (sl, ssl, rb) in ((qs, sqq, rnq_b), (ks, sqk, rnk_b)):
                ps, pb = next_psf()
                P.mm(ps[:, :], [(ones_b[:], sq[:, ssl, :].rearrange("p h t -> p (h t)"))], reads=[sq_b[j], cb], writes=[pb])
                act(rn[:, sl, :], v3(ps), AF.Ln, [pb, rb], [rb], bias=EPS)
                act(rn[:, sl, :], rn[:, sl, :], AF.Exp, [rb], [rb], scale=-0.5)
                yield
            stt(qh[:, hs, :], cv[:, qs, :], 128.0 ** -0.5, rn[:, qs, :], ALU.mult, ALU.mult, cvq + [rnq_b], [qh_b[j]])
            tt("dve", kh[:, hs, :], cv[:, ks, :], rn[:, ks, :], ALU.mult, cvk + [rnk_b], [kh_b[j]])
            yield
            for (src, src_b, dst, dst_b) in ((kh, kh_b, ktm, ktm_b), (vT, vT_b, vtm, vtm_b)):
                pst, pstb = next_psb()

                def fn(e, src=src, pst=pst):
                    ins = None
                    for hh in range(4):
                        ins = e.transpose(out=pst[:, hh * 128:(hh + 1) * 128], in_=src[:, j * 4 + hh, :], identity=ident_b[:])
                    return ins
                P.op("pe", fn, reads=[src_b[j], cb], writes=[pstb])
                act(dst[:, hs, :], pst[:, 0:512].rearrange("p (h t) -> p h t", h=4), AF.Copy, [pstb], [dst_b[j]])
                yield
            tt("pool", vbt[:, hs, :], vtm[:, hs, :], bc4(sc, 0, j), ALU.mult, [vtm_b[j], sc_b], [vbt_b[j]])
            tt("pool", kbg[:, hs, :], ktm[:, hs, :], bc4(sc, 32, j), ALU.mult, [ktm_b[j], sc_b], [kbg_b[j]])
            tt("pool", kst[:, hs, :], ktm[:, hs, :], bc4(sc, 40, j), ALU.mult, [ktm_b[j], sc_b], [kst_b[j]])
            yield
            P.op("dve", lambda e: e.tensor_copy(out=gbm[:, hs, :], in_=bc4(sc, 8, j)), reads=[sc_b], writes=[gbm_b[j]])
            pg, pgb = next_psf()
            P.mms([(pg[:, hh * 128:(hh + 1) * 128], gbm[:, j * 4 + hh, :], cst(K_U64), True, True) for hh in range(4)],
                  reads=[gbm_b[j], cb], writes=[pgb])
            yield
            act(EG[:, hs, :], v3(pg), AF.Exp, [pgb], [EG_b[j]])
            tt("dve", MG[:, hs, :], v3(pg), cbc(K_NEGT), ALU.add, [pgb, cb], [MG_b[j]])
            stt(MG2[:, hs, :], v3(pg), -1.0, cbc(K_NEGS), ALU.mult, ALU.add, [pgb, cb], [MG2_b[j]])
            yield
            tt("dve", MG[:, hs, :], MG[:, hs, :], bc4(sc, 16, j), ALU.subtract, [MG_b[j], sc_b], [MG_b[j]])
            act(Et[:, hs, :], MG[:, hs, :], AF.Exp, [MG_b[j]], [Et_b[j]])
            tt("dve", MG2[:, hs, :], MG2[:, hs, :], bc4(sc, 16, j), ALU.add, [MG2_b[j], sc_b], [MG2_b[j]])
            act(En[:, hs, :], MG2[:, hs, :], AF.Exp, [MG2_b[j]], [En_b[j]])
            yield
            tt("pool", En[:, hs, :], En[:, hs, :], bc4(sc, 24, j), ALU.mult, [En_b[j], sc_b], [En_b[j]])
            tt("pool", qd[:, hs, :], qh[:, hs, :], EG[:, hs, :], ALU.mult, [qh_b[j], EG_b[j]], [qd_b[j]])
            yield
            pa, pab = next_psf()
            P.mms([(pa[:, hh * 128:(hh + 1) * 128], kh[:, j * 4 + hh, :], kh[:, j * 4 + hh, :], True, True) for hh in range(4)],
                  reads=[kh_b[j]], writes=[pab])
            pq, pqb = next_psf()
            P.mms([(pq[:, hh * 128:(hh + 1) * 128], kh[:, j * 4 + hh, :], qh[:, j * 4 + hh, :], True, True) for hh in range(4)],
                  reads=[kh_b[j], qh_b[j]], writes=[pqb])
            yield
            tt("dve", Bm[:, hs, :], v3(pa), En[:, hs, :], ALU.mult, [pab, En_b[j]], [Bm_b[j]])
            tt("dve", qkT[:, hs, :], v3(pq), Et[:, hs, :], ALU.mult, [pqb, Et_b[j]], [qkT_b[j]])
            yield
            if INV_BF16:
                pt, ptb = next_psb()
                ptv = pt[:, 0:512].rearrange("p (h t) -> p h t", h=4)
                idt = ident_b[:]
            else:
                pt, ptb = next_psf()
                ptv = v3(pt)
                idt = cst(K_ID)

            def fnt(e, pt=pt, idt=idt):
                ins = None
                for hh in range(4):
                    ins = e.transpose(out=pt[:, hh * 128:(hh + 1) * 128], in_=Bm[:, j * 4 + hh, :], identity=idt)
                return ins
            P.op("pe", fnt, reads=[Bm_b[j], cb], writes=[ptb])
            act(Bt[:, hs, :], ptv, AF.Copy, [ptb], [Bt_b[j]])
            tt("dve", RT[:, hs, :], ptv, cbc(K_ID), ALU.add, [ptb, cb], [RT_b[j]])
            yield
            Pc, Pc_b, PTc, PTc_b = Bm, Bm_b, Bt, Bt_b
            Pn, Pn_b, PTn, PTn_b = Bm2, Bm2_b, Bt2, Bt2_b
            for lvl in range(1, 6):
                last = lvl == 5
                pp, ppb = next_psf()
                P.mms([(pp[:, hh * 128:(hh + 1) * 128], PTc[:, j * 4 + hh, :], Pc[:, j * 4 + hh, :], True, True) for hh in range(4)],
                      reads=[Pc_b[j], PTc_b[j]], writes=[ppb])
                if not last:
                    p2, p2b = next_psf()
                    P.mms([(p2[:, hh * 128:(hh + 1) * 128], Pc[:, j * 4 + hh, :], PTc[:, j * 4 + hh, :], True, True) for hh in range(4)],
                          reads=[Pc_b[j], PTc_b[j]], writes=[p2b])
                yield
                act(Pn[:, hs, :], v3(pp), AF.Copy, [ppb], [Pn_b[j]])
                if not last:
                    act(PTn[:, hs, :], v3(p2), AF.Copy, [p2b], [PTn_b[j]])
                yield
                pr, prb = next_psf()
                P.mms([(pr[:, hh * 128:(hh + 1) * 128], Pn[:, j * 4 + hh, :], RT[:, j * 4 + hh, :], True, True) for hh in range(4)],
                      reads=[Pn_b[j], RT_b[j]], writes=[prb])
                yield
                tt("dve", RT[:, hs, :], RT[:, hs, :], v3(pr), ALU.add, [prb, RT_b[j]], [RT_b[j]])
                yield
                Pc, Pc_b, PTc, PTc_b, Pn, Pn_b, PTn, PTn_b = Pn, Pn_b, PTn, PTn_b, Pc, Pc_b, PTc, PTc_b
            if not INV_BF16:
                act(RTb[:, hs, :], RT[:, hs, :], AF.Copy, [RT_b[j]], [RTb_b[j]])
                yield
            pu, pub = next_psf()
            P.mms([(pu[:, hh * 128:(hh + 1) * 128], RTb[:, j * 4 + hh, :], vbt[:, j * 4 + hh, :], True, True) for hh in range(4)],
                  reads=[RTb_b[j], vbt_b[j]], writes=[pub])
            pw, pwb = next_psf()
            P.mms([(pw[:, hh * 128:(hh + 1) * 128], kbg[:, j * 4 + hh, :], RTb[:, j * 4 + hh, :], True, True) for hh in range(4)],
                  reads=[RTb_b[j], kbg_b[j]], writes=[pwb])
            yield
            act(us[:, hs, :], v3(pu), AF.Copy, [pub], [us_b[j]])
            act(wT[:, hs, :], v3(pw), AF.Copy, [pwb], [wT_b[j]])
            yield
            tt("pool", DG[:, hs, :], gbt[:, j * 512:(j + 1) * 512].rearrange("p (h t) -> p h t", h=4),
               dhn[:].unsqueeze(1).to_broadcast([128, 4, 128]), ALU.mult, [gbt_b, wb], [DG_b[j]])
            for c in range(2):
                rows = slice(c * 64, (c + 1) * 64)
                gcol = c * 64 + 63
                pa, pab = next_psf()
                P.mms([(pa[rows, hh * 128:(hh + 1) * 128], wT[:, j * 4 + hh, rows], Sb[:, j * 4 + hh, :], True, True) for hh in range(4)],
                      reads=[wT_b[j], Sb_b[j]], writes=[pab])
                yield
                tt("dve", vn[rows, hs, :], us[rows, hs, :], pa[rows, :].rearrange("p (h t) -> p h t", h=4), ALU.subtract,
                   [pab, us_b[j]], [vn_b[j]])
                yield
                po, pob = next_psf()
                items = []
                for hh in range(4):
                    h = j * 4 + hh
                    items.append((po[rows, hh * 128:(hh + 1) * 128], qd[:, h, rows], Sb[:, h, :], True, False))
                    items.append((po[rows, hh * 128:(hh + 1) * 128], qkT[rows, h, rows], vn[rows, h, :], False, True))
                P.mms(items, reads=[qd_b[j], Sb_b[j], qkT_b[j], vn_b[j]], writes=[pob])
                pS, pSb = next_psf()
                P.mms([(pS[:, hh * 128:(hh + 1) * 128], kst[rows, j * 4 + hh, :], vn[rows, j * 4 + hh, :], True, True) for hh in range(4)],
                      reads=[kst_b[j], vn_b[j]], writes=[pSb])
                tt("dve", S[:, hs, :], S[:, hs, :], EG[:, hs, gcol:gcol + 1].to_broadcast([128, 4, 128]), ALU.mult,
                   [S_b[j], EG_b[j]], [S_b[j]])
                yield
                act(osb[rows, hs, :], po[rows, :].rearrange("p (h t) -> p h t", h=4), AF.Copy, [pob], [osb_b[j]])
                tt("dve", S[:, hs, :], S[:, hs, :], v3(pS), ALU.add, [S_b[j], pSb], [S_b[j]])
                yield
                act(Sb[:, hs, :], S[:, hs, :], AF.Copy, [S_b[j]], [Sb_b[j]])
                yield
            tt("pool", us[:, hs, :], osb[:, hs, :], osb[:, hs, :], ALU.mult, [osb_b[j], us_b[j]], [us_b[j]])
            yield
            P.op("dve", lambda e: e.tensor_reduce(out=ss2[:, hs], in_=us[:, hs, :], axis=AX.X, op=ALU.add),
                 reads=[us_b[j]], writes=[ss2_b[j]])
            yield
            act(ss2[:, hs], ss2[:, hs], AF.Ln, [ss2_b[j]], [ss2_b[j]], bias=EPS, scale=1.0 / 128.0)
            act(ss2[:, hs], ss2[:, hs], AF.Exp, [ss2_b[j]], [ss2_b[j]], scale=-0.5)
            yield
            tt("dve", osb[:, hs, :], osb[:, hs, :], ss2[:, hs].unsqueeze(2).to_broadcast([128, 4, 128]), ALU.mult,
               [osb_b[j], ss2_b[j]], [osb_b[j]])
            yield
            tt("dve", osb[:, hs, :], osb[:, hs, :], DG[:, hs, :], ALU.mult, [osb_b[j], DG_b[j]], [osb_b[j]])
            yield
            tt("dve", yo[:, j * 512:(j + 1) * 512], osb[:, hs, :].rearrange("p h t -> p (h t)"), yat[:, j * 512:(j + 1) * 512],
               ALU.add, [osb_b[j], yat_b], [yo_b[j]])
            P.dma("sp", y_d[r0:r0 + TT, j * 512:(j + 1) * 512], yo[:, j * 512:(j + 1) * 512], dsem(f"st{j}"), reads=[yo_b[j]])
            yield

        def run_rr(gens):
            gens = list(gens)
            while gens:
                for g in list(gens):
                    try:
                        next(g)
                    except StopIteration:
                        gens.remove(g)

        def loads(t):
            r0 = t * TT
            P.dma("sp", yat[:], ya_d[r0:r0 + TT, :], dsem("lda0"), writes=[yat_b])
            P.dma("sp", gbt[:], gb_d[r0:r0 + TT, :], dsem("ldg0"), writes=[gbt_b])

        run_rr([front(0)])
        for t in range(ntiles):
            loads(t)
            h0, h1 = half(t, 0), half(t, 1)
            for _ in range(6):
                next(h0)
                next(h1)
            gens = [h0, h1]
            if t + 1 < ntiles:
                gens.append(front(t + 1))
            run_rr(gens)

    ffn_stage(x_d, h1_d, wg1_d, wu1_d, wd1_d, PK_N1, final=False, with_mix=False)
    if upto >= 2:
        gla_stage()
    if upto >= 3:
        dn_stage()
    if upto >= 4:
        ffn_stage(h1_d, out_d, wg2_d, wu2_d, wd2_d, PK_N2, final=True, with_mix=True)

    with nc.Block() as block:
        P.emit(eng_sems, block, list(dsems.values()))
    return nc


def make_consts():
    i = np.arange(128)
    same = (i[:, None] // 64) == (i[None, :] // 64)
    c = np.zeros((8, 128, 128), np.float32)
    c[K_ID] = np.eye(128)
    c[K_UQ] = np.where(i[:, None] <= i[None, :], -1.0 / 16.0, 0.0)
    c[K_US] = np.where(i[:, None] > i[None, :], -1.0 / 16.0, 0.0)
    c[K_CM] = np.where(i[:, None] <= i[None, :], 1.0, 0.0)
    c[K_U64] = np.where((i[:, None] <= i[None, :]) & same, 1.0, 0.0)
    c[K_USUF] = np.where((i[:, None] > i[None, :]) & same, 1.0, 0.0)
    c[K_NEGT] = np.where((i[None, :] >= i[:, None]) & same, 0.0, NEG)
    c[K_NEGS] = np.where((i[None, :] < i[:, None]) & same, 0.0, NEG)
    return np.ascontiguousarray(c.transpose(1, 0, 2).reshape(128, 8 * 128))


def pack_params(inp):
    f = lambda a: np.asarray(a, np.float32)
    pk = np.zeros((128, PK_END), np.float32)
    pk[:, PK_N1:PK_N1 + 8] = f(inp["ffn1_norm"])[0].reshape(8, 128).T
    pk[:, PK_NM:PK_NM + 8] = f(inp["mix_norm"])[0].reshape(8, 128).T
    pk[:, PK_N2:PK_N2 + 8] = f(inp["ffn2_norm"])[0].reshape(8, 128).T
    cw = f(inp["conv_w"])[0]
    pk[:, PK_CONV:PK_CONV + 96] = cw.reshape(4, 24, 128).transpose(2, 1, 0).reshape(128, 96)
    pk[:, PK_ALOG:PK_ALOG + 8] = f(inp["dn_a_log"])[0][None, :]
    pk[:, PK_DTB:PK_DTB + 8] = f(inp["dn_dt_bias"])[0][None, :]
    rows = np.zeros((1, RW_END), np.float32)
    rows[0, RW_B2:RW_B2 + 512] = f(inp["b_gla_gate"])[0]
    rows[0, RW_GHN:RW_GHN + 256] = f(inp["gla_head_norm"])[0]
    rows[0, RW_DHN:RW_DHN + 128] = f(inp["dn_head_norm"])[0]
    rows[0, RW_FIN:RW_FIN + 1024] = f(inp["final_norm"])
    return pk, rows


def shared_maps(inp):
    f = lambda a: np.ascontiguousarray(np.asarray(a, np.float32))
    pk, rows = pack_params(inp)
    return {
        "wg1": f(inp["ffn1_w_gate"][0]), "wu1": f(inp["ffn1_w_up"][0]), "wd1": f(inp["ffn1_w_down"][0]),
        "wg2": f(inp["ffn2_w_gate"][0]), "wu2": f(inp["ffn2_w_up"][0]), "wd2": f(inp["ffn2_w_down"][0]),
        "w_in": f(inp["w_in"][0]), "w_out": f(inp["w_out"][0]), "w2": f(inp["w_gla_gate"][0]),
        "consts": make_consts(), "pk": pk, "rows": rows,
    }


def kernel(**inputs):
    x = np.asarray(inputs["x"], np.float32)
    shared = shared_maps(inputs)
    NT = SEQ
    nc = build_program(NT)
    in_maps = []
    for c in range(N_CORES):
        m = dict(shared)
        m["x"] = np.ascontiguousarray(x[c % BATCH])
        in_maps.append(m)
    res = run_bass_kernel_spmd(nc, in_maps, core_ids=list(range(N_CORES)))
    out = np.stack([np.asarray(res.results[b]["out"], np.float32) for b in range(BATCH)], axis=0)
    return out
```

```python
import numpy as np
import concourse.bass as bass
import concourse.mybir as mybir
from concourse.bass_utils import run_bass_kernel_spmd

F32 = mybir.dt.float32
BF16 = mybir.dt.bfloat16
AF = mybir.ActivationFunctionType
ALU = mybir.AluOpType
AX = mybir.AxisListType

D = 1024
KC = 8
DFF = 2816
FC = 22
D_IN = 9248
EPS = 1e-6
NEG = -30000.0
N_CORES = 8
SEQ = 8192
BATCH = 4

C_GQ, C_GK, C_GV, C_GR, C_GLR = 0, 512, 1024, 2048, 3072
C_DQ, C_DK, C_DV, C_DG, C_DB, C_DA = 3088, 4112, 5136, 6160, 7184, 7192
C_MA, C_MB = 7200, 8224

K_ID, K_UQ, K_US, K_CM, K_U64, K_USUF, K_NEGT, K_NEGS = range(8)
PK_N1, PK_NM, PK_N2, PK_CONV, PK_ALOG, PK_DTB, PK_END = 0, 8, 16, 24, 120, 128, 136
RW_B2, RW_GHN, RW_DHN, RW_FIN, RW_END = 0, 512, 768, 896, 1920


class Buf:
    __slots__ = ("name", "w", "r", "rd", "excl")

    def __init__(self, name, excl=False):
        self.name = name
        self.excl = excl
        self.w = None
        self.r = {}
        self.rd = []


class DSem:
    def __init__(self, h):
        self.h = h
        self.count = 0


class Op:
    __slots__ = ("eng", "fn", "deps", "sem", "value", "signal", "is_dma")


class Prog:
    def __init__(self, nc):
        self.nc = nc
        self.streams = {k: [] for k in ("pe", "act", "dve", "pool", "sp")}
        self.fence = []
        self.fence_id = 0
        self.seen = {k: 0 for k in self.streams}
        self.dma_recent = []
        self.out_dmas = []

    def op(self, eng, fn, reads=(), writes=(), dsem=None):
        o = Op()
        o.eng, o.fn, o.signal, o.is_dma = eng, fn, False, dsem is not None
        o.sem, o.value = None, 0
        deps = {}
        for b in reads:
            if b.w is not None:
                deps[id(b.w)] = b.w
            if b.excl:
                for k2, x in b.r.items():
                    if k2 != eng:
                        deps[id(x)] = x
        for b in writes:
            if b.w is not None:
                deps[id(b.w)] = b.w
            for x in b.r.values():
                deps[id(x)] = x
            for x in b.rd:
                deps[id(x)] = x
        if self.seen[eng] < self.fence_id:
            for x in self.fence:
                deps[id(x)] = x
            self.seen[eng] = self.fence_id
        o.deps = list(deps.values())
        for b in reads:
            if o.is_dma:
                b.rd.append(o)
            else:
                b.r[eng] = o
        for b in writes:
            b.w = o
            b.r = {}
            b.rd = []
        if o.is_dma:
            dsem.count += 16
            o.sem, o.value = dsem, dsem.count
            self.dma_recent.append(o)
        self.streams[eng].append(o)
        return o

    def barrier(self):
        f = []
        for k, s in self.streams.items():
            for o in reversed(s):
                if not o.is_dma:
                    f.append(o)
                    break
        f.extend(self.dma_recent)
        self.dma_recent = []
        self.fence = f
        self.fence_id += 1

    def dma(self, eng, out, in_, dsem, reads=(), writes=(), is_out=False):
        o = self.op(eng, lambda e: e.dma_start(out=out, in_=in_), reads, writes, dsem=dsem)
        if is_out:
            self.out_dmas.append(o)
        return o

    def mms(self, items, reads, writes):
        def fn(e):
            ins = None
            for (o, l, r, st, sp) in items:
                ins = e.matmul(o, lhsT=l, rhs=r, start=st, stop=sp)
            return ins
        return self.op("pe", fn, reads, writes)

    def mm(self, out, pairs, reads, writes):
        n = len(pairs)
        return self.mms([(out, l, r, i == 0, i == n - 1) for i, (l, r) in enumerate(pairs)],
                        reads, writes)

    def emit(self, eng_sems, block, all_dsems=()):
        for s in self.streams.values():
            for o in s:
                for d in o.deps:
                    d.signal = True
        for o in self.out_dmas:
            o.signal = True
        for k, s in self.streams.items():
            cnt = 0
            for o in s:
                if o.is_dma:
                    continue
                if o.signal:
                    cnt += 1
                    o.value = cnt
                    o.sem = eng_sems[k]

        final = list(self.out_dmas)

        def run(e, k):
            waited = {}
            for o in self.streams[k]:
                need = {}
                for d in o.deps:
                    if (not d.is_dma) and d.eng == "pe" and k == "pe":
                        continue
                    if SKIP_SAME and (not d.is_dma) and d.eng == k:
                        continue
                    key = id(d.sem)
                    if waited.get(key, 0) < d.value and need.get(key, (None, 0))[1] < d.value:
                        need[key] = (d.sem, d.value)
                for key, (sm, v) in need.items():
                    e.wait_ge(sm.h, v)
                    waited[key] = v
                ins = o.fn(e)
                if o.is_dma:
                    ins.then_inc(o.sem.h, 16)
                elif o.signal:
                    ins.then_inc(o.sem.h, 1)
            if k == "sp":
                for sm in all_dsems:
                    if sm.count > 0:
                        e.wait_ge(sm.h, sm.count)

        @block.sync
        def _(e):
            run(e, "sp")

        @block.tensor
        def _(e):
            run(e, "pe")

        @block.scalar
        def _(e):
            run(e, "act")

        @block.vector
        def _(e):
            run(e, "dve")

        @block.gpsimd
        def _(e):
            run(e, "pool")


class Arena:
    def __init__(self, nc, start=16640, end=229376):
        self.nc, self.start, self.end, self.cur, self.n = nc, start, end, start, 0

    def alloc(self, name, shape, dt):
        esz = 4 if dt == F32 else 2
        per = esz
        for s in shape[1:]:
            per *= s
        per = (per + 31) // 32 * 32
        assert self.cur + per <= self.end, (name, self.cur, per, self.end)
        self.n += 1
        t = self.nc.alloc_sbuf_tensor_at(f"{name}_{self.n}", list(shape), dt, offset=self.cur)
        self.last_off = self.cur
        self.cur += per
        return t

    def alias(self, name, shape, dt, off):
        self.n += 1
        return self.nc.alloc_sbuf_tensor_at(f"{name}_{self.n}", list(shape), dt, offset=off)

    def mark(self):
        return self.cur

    def reset(self, m):
        self.cur = m


DN_CUT = 0
EGV = 0
SER = 0
INV_BF16 = 1
SKIP_SAME = 0


def build_program(NT, debug=False, upto=4):
    nc = bass.Bass("TRN2", target_bir_lowering=False)
    P = Prog(nc)
    A = Arena(nc)

    def din(name, shape):
        return nc.dram_tensor(name, list(shape), F32, kind="ExternalInput").ap()

    x_d = din("x", [NT, D])
    wg1_d, wu1_d, wd1_d = din("wg1", [D, DFF]), din("wu1", [D, DFF]), din("wd1", [DFF, D])
    wg2_d, wu2_d, wd2_d = din("wg2", [D, DFF]), din("wu2", [D, DFF]), din("wd2", [DFF, D])
    win_d, wout_d = din("w_in", [D, D_IN]), din("w_out", [D, D])
    w2_d = din("w2", [16, 512])
    consts_d = din("consts", [128, 8 * 128])
    pk_d = din("pk", [128, PK_END])
    rows_d = din("rows", [1, RW_END])
    out_d = nc.dram_tensor("out", [NT, D], F32, kind="ExternalOutput").ap()
    skind = "ExternalOutput" if debug else "Internal"
    h1_d = nc.dram_tensor("h1", [NT, D], F32, kind=skind).ap()
    ya_d = nc.dram_tensor("ya", [NT, D], F32, kind=skind).ap()
    y_d = nc.dram_tensor("yy", [NT, D], BF16, kind=skind).ap()
    gb_d = nc.dram_tensor("gbd", [NT, D], F32, kind="Internal").ap()

    psf = [nc.alloc_psum_tensor(f"psf{i}", [128, 512], F32) for i in range(7)]
    psb = [nc.alloc_psum_tensor("psb0", [128, 1024], BF16)]
    psf_b = [Buf(f"psf{i}", excl=True) for i in range(7)]
    psb_b = [Buf("psb0", excl=True)]

    class PsPool:
        def __init__(self, idxs):
            self.idxs, self.i = list(idxs), 0

        def next(self):
            k = self.idxs[self.i % len(self.idxs)]
            self.i += 1
            return psf[k], psf_b[k]

        def bank(self, j):
            k = self.idxs[j]
            return psf[k], psf_b[k]

    pool_box = [PsPool(range(7))]

    def next_psf():
        return pool_box[0].next()

    def next_psb():
        return psb[0], psb_b[0]

    sem_ctx = []

    def new_sem(name):
        cm = nc.semaphore(name)
        h = cm.__enter__()
        sem_ctx.append(cm)
        return h

    eng_sems = {k: DSem(new_sem("s_" + k)) for k in ("pe", "act", "dve", "pool")}
    dsems = {}

    def dsem(name):
        if name not in dsems:
            dsems[name] = DSem(new_sem("d_" + name))
        return dsems[name]

    consts = A.alloc("consts", [128, 8 * 128], F32)
    ident_b = A.alloc("identb", [128, 128], BF16)
    ones_b = A.alloc("onesb", [128, 128], BF16)
    pk = A.alloc("pk", [128, PK_END], F32)
    cb = Buf("consts")
    P.dma("sp", consts[:], consts_d, dsem("c0"), writes=[cb])
    P.dma("sp", pk[:], pk_d, dsem("c0"), writes=[cb])
    P.dma("pool", ident_b[:], consts_d[:, 0:128], dsem("c1"), writes=[cb])
    P.op("pool", lambda e: e.memset(ones_b[:], 1.0), writes=[cb])

    def cst(i):
        return consts[:, i * 128:(i + 1) * 128]

    persist_mark = A.mark()

    def rms_scale(src_tile, src_buf, TB, ss, rstd, sb, junk, jb, ncols=D, nsub=1):
        n = TB * nsub
        for j in range(n):
            tb, sub = divmod(j, nsub)
            w = ncols // nsub if nsub > 1 else ncols
            src = src_tile[:, tb, sub * w:(sub + 1) * w]
            P.op("act", lambda e, src=src, j=j, w=w: e.activation(
                out=junk[:, 0:w], in_=src, func=AF.Square, accum_out=ss[:, j:j + 1]),
                reads=[src_buf], writes=[jb, sb])
        wdt = float(ncols // nsub if nsub > 1 else ncols)
        P.op("act", lambda e: e.activation(out=rstd[:, 0:n], in_=ss[:, 0:n], func=AF.Ln, bias=EPS, scale=1.0 / wdt),
             reads=[sb], writes=[sb])
        P.op("act", lambda e: e.activation(out=rstd[:, 0:n], in_=rstd[:, 0:n], func=AF.Exp, scale=-0.5),
             reads=[sb], writes=[sb])

    def norm_transpose(xt, xt_b, TB, xn, xn_b, nT, nT_b, pk_off, ss, rstd, sb, junk, jb):
        TT = TB * 128
        rms_scale(xt, xt_b, TB, ss, rstd, sb, junk, jb)
        for tb in range(TB):
            P.op("act", lambda e, tb=tb: e.activation(out=xn[:, tb, :], in_=xt[:, tb, :], func=AF.Copy,
                                                      scale=rstd[:, tb:tb + 1]),
                 reads=[xt_b, sb], writes=[xn_b])
        for half in range(2):
            pst, pstb = next_psb()

            def fn(e, half=half, pst=pst):
                ins = None
                for kk in range(4):
                    k = half * 4 + kk
                    for tb in range(TB):
                        ins = e.transpose(out=pst[:, kk * TT + tb * 128: kk * TT + (tb + 1) * 128],
                                          in_=xn[:, tb, k * 128:(k + 1) * 128], identity=ident_b[:])
                return ins
            P.op("pe", fn, reads=[xn_b, cb], writes=[pstb])
            P.op("dve", lambda e, half=half, pst=pst: e.tensor_tensor(
                out=nT[:, half * 4:half * 4 + 4, :],
                in0=pst[:, 0:4 * TT].rearrange("p (k t) -> p k t", k=4),
                in1=pk[:, pk_off + half * 4: pk_off + half * 4 + 4].unsqueeze(2).to_broadcast([128, 4, TT]),
                op=ALU.mult), reads=[pstb, cb], writes=[nT_b])

    def load_w(dst, dst_b, src_d, c0, c1, name, rows_per=128):
        nk = src_d.shape[0] // 128
        for k in range(nk):
            P.dma("pool", dst[:, k, 0:c1 - c0], src_d[k * 128:(k + 1) * 128, c0:c1], dsem(name), writes=[dst_b])

    def ffn_stage(src_d, dst_d, wg_d, wu_d, wd_d, pk_off, final, with_mix):
        P.barrier()
        A.reset(persist_mark)
        TB = 2
        TT = 256
        ntiles = NT // TT
        Wg = A.alloc("Wg", [128, KC, DFF], BF16)
        Wu = A.alloc("Wu", [128, KC, DFF], BF16)
        Wd = A.alloc("Wd", [128, FC, D], BF16)
        wb = Buf("ffnw")
        wgu_b = [Buf("wgu0"), Buf("wgu1")]
        wd_b = Buf("wd")
        HC = 11 * 128
        if with_mix:
            Wo = A.alloc("Wo", [128, KC, D], BF16)
            load_w(Wo, wb, wout_d, 0, D, "w0")
        for hf in range(2):
            for (dst, src) in ((Wg, wg_d), (Wu, wu_d)):
                for k in range(KC):
                    P.dma("pool", dst[:, k, hf * HC:(hf + 1) * HC], src[k * 128:(k + 1) * 128, hf * HC:(hf + 1) * HC],
                          dsem(f"wg{hf}"), writes=[wgu_b[hf]])
        load_w(Wd, wd_b, wd_d, 0, D, "w1")
        if with_mix:
            yt1 = A.alloc("yt", [128, TB, D], BF16)
            yt = [yt1, yt1]
            yb1 = Buf("yt")
            yt_b = [yb1, yb1]
            yT = A.alloc("yT", [128, KC, TT], BF16)
            yT_b = Buf("yT")
        if final:
            fin = A.alloc("fin", [128, D], F32)
            P.dma("sp", fin[:], rows_d[:, RW_FIN:RW_FIN + D].partition_broadcast(128), dsem("c0"), writes=[wb])
        xt = [A.alloc("xt", [128, TB, D], F32) for _ in range(2)]
        xt_b = [Buf("xt") for _ in range(2)]
        xn = A.alloc("xn", [128, TB, D], BF16)
        xn_b = Buf("xn")
        nT = [A.alloc("nT", [128, KC, TT], BF16) for _ in range(2)]
        nT_b = [Buf("nT") for _ in range(2)]
        hT = A.alloc("hT", [128, FC, TT], BF16)
        hT_b = Buf("hT")
        sil = [A.alloc("sil", [128, TT], F32) for _ in range(2)]
        sil_b = [Buf("sil") for _ in range(2)]
        junk = xn[:, 0, :]
        jb = xn_b
        ss = A.alloc("ss", [128, 4], F32)
        rstd = A.alloc("rstd", [128, 4], F32)
        sb = Buf("ss")

        def prologue(t):
            s = t % 2
            r0 = t * TT
            P.dma("sp", xt[s][:], src_d[r0:r0 + TT, :].rearrange("(tb p) d -> p tb d", p=128),
                  dsem(f"ld{s}"), writes=[xt_b[s]])
            if with_mix:
                P.dma("sp", yt[s][:], y_d[r0:r0 + TT, :].rearrange("(tb p) d -> p tb d", p=128),
                      dsem("ldy"), writes=[yt_b[s]])
                for half in range(2):
                    pst, pstb = next_psb()

                    def fn(e, half=half, pst=pst, s=s):
                        ins = None
                        for kk in range(4):
                            k = half * 4 + kk
                            for tb in range(TB):
                                ins = e.transpose(out=pst[:, kk * TT + tb * 128: kk * TT + (tb + 1) * 128],
                                                  in_=yt[s][:, tb, k * 128:(k + 1) * 128], identity=ident_b[:])
                        return ins
                    P.op("pe", fn, reads=[yt_b[s], cb], writes=[pstb])
                    P.op("act", lambda e, half=half, pst=pst: e.activation(
                        out=yT[:, half * 4:half * 4 + 4, :],
                        in_=pst[:, 0:4 * TT].rearrange("p (k t) -> p k t", k=4), func=AF.Copy),
                        reads=[pstb], writes=[yT_b])
                for tb in range(TB):
                    for ch in range(2):
                        ps, psb_ = next_psf()
                        P.mm(ps[:, :], [(yT[:, k, tb * 128:(tb + 1) * 128], Wo[:, k, ch * 512:(ch + 1) * 512])
                                        for k in range(KC)], reads=[yT_b, wb], writes=[psb_])
                        P.op("dve", lambda e, ps=ps, tb=tb, ch=ch, s=s: e.tensor_tensor(
                            out=xt[s][:, tb, ch * 512:(ch + 1) * 512], in0=ps[:, :],
                            in1=xt[s][:, tb, ch * 512:(ch + 1) * 512], op=ALU.add),
                            reads=[psb_, xt_b[s]], writes=[xt_b[s]])
            norm_transpose(xt[s], xt_b[s], TB, xn, xn_b, nT[s], nT_b[s], pk_off, ss, rstd, sb, junk, jb)

        prologue(0)
        for t in range(ntiles):
            s = t % 2
            r0 = t * TT
            for f in range(FC):
                ps, psb_ = next_psf()
                items = []
                for k in range(KC):
                    items.append((ps[:, 0:TT], Wg[:, k, f * 128:(f + 1) * 128], nT[s][:, k, :], k == 0, k == KC - 1))
                for k in range(KC):
                    items.append((ps[:, TT:2 * TT], Wu[:, k, f * 128:(f + 1) * 128], nT[s][:, k, :], k == 0, k == KC - 1))
                P.mms(items, reads=[wgu_b[f // 11], nT_b[s]], writes=[psb_])
                s2 = f % 2
                P.op("act", lambda e, ps=ps, s2=s2: e.activation(out=sil[s2][:], in_=ps[:, 0:TT], func=AF.Silu),
                     reads=[psb_], writes=[sil_b[s2]])
                P.op("dve", lambda e, ps=ps, s2=s2, f=f: e.tensor_tensor(
                    out=hT[:, f, :], in0=sil[s2][:], in1=ps[:, TT:2 * TT], op=ALU.mult),
                    reads=[psb_, sil_b[s2]], writes=[hT_b])
            if t + 1 < ntiles:
                prologue(t + 1)
            for tb in range(TB):
                for ch in range(2):
                    ps, psb_ = next_psf()
                    P.mm(ps[:, :], [(hT[:, f, tb * 128:(tb + 1) * 128], Wd[:, f, ch * 512:(ch + 1) * 512])
                                    for f in range(FC)], reads=[hT_b, wd_b], writes=[psb_])
                    P.op("dve", lambda e, ps=ps, tb=tb, ch=ch, s=s: e.scalar_tensor_tensor(
                        out=xt[s][:, tb, ch * 512:(ch + 1) * 512], in0=ps[:, :], scalar=0.5,
                        in1=xt[s][:, tb, ch * 512:(ch + 1) * 512], op0=ALU.mult, op1=ALU.add),
                        reads=[psb_, xt_b[s]], writes=[xt_b[s]])
            if final:
                rms_scale(xt[s], xt_b[s], TB, ss, rstd, sb, junk, jb)
                for tb in range(TB):
                    P.op("dve", lambda e, tb=tb, s=s: e.scalar_tensor_tensor(
                        out=xt[s][:, tb, :], in0=xt[s][:, tb, :], scalar=rstd[:, tb:tb + 1],
                        in1=fin[:], op0=ALU.mult, op1=ALU.mult),
                        reads=[xt_b[s], sb, wb], writes=[xt_b[s]])
            P.dma("sp", dst_d[r0:r0 + TT, :].rearrange("(tb p) d -> p tb d", p=128), xt[s][:],
                  dsem(f"st{s}"), reads=[xt_b[s]], is_out=final)

    def gla_stage():
        P.barrier()
        A.reset(persist_mark)
        TB, TT = 2, 256
        ntiles = NT // TT
        NW = C_GLR + 16
        Wa = A.alloc("Wa", [128, KC, NW], BF16)
        Wm = A.alloc("Wm", [128, KC, D], BF16)
        Wdg = A.alloc("Wdg", [128, KC, D], BF16)
        Wmb = A.alloc("Wmb", [128, KC, D], BF16)
        wb = Buf("glaw")
        wb2 = Buf("glaw2")
        load_w(Wa, wb, win_d, 0, NW, "w0")
        load_w(Wm, wb2, win_d, C_MA, C_MA + D, "w1")
        load_w(Wdg, wb2, win_d, C_DG, C_DG + D, "w1")
        load_w(Wmb, wb2, win_d, C_MB, C_MB + D, "w1")
        w2 = A.alloc("w2", [16, 512], F32)
        b2 = A.alloc("b2", [1, 512], F32)
        ones_r = A.alloc("onesr", [1, 128], F32)
        ghn = A.alloc("ghn", [128, 256], F32)
        P.dma("sp", w2[:], w2_d, dsem("c0"), writes=[wb])
        P.dma("sp", b2[:], rows_d[:, RW_B2:RW_B2 + 512], dsem("c0"), writes=[wb])
        P.dma("sp", ghn[:], rows_d[:, RW_GHN:RW_GHN + 256].partition_broadcast(128), dsem("c0"), writes=[wb])
        P.op("pool", lambda e: e.memset(ones_r[:], 1.0), writes=[wb])
        S = A.alloc("S", [128, 4, 256], F32)
        Sb = A.alloc("Sb", [128, 4, 256], BF16)
        S_b, Sb_b = Buf("S"), Buf("Sb")
        P.op("pool", lambda e: e.memset(S[:], 0.0), writes=[S_b])
        P.op("pool", lambda e: e.memset(Sb[:], 0.0), writes=[Sb_b])

        ht, ht_b = A.alloc("ht", [128, TB, D], F32), Buf("ht")
        xn, xn_b = A.alloc("xn", [128, TB, D], BF16), Buf("xn")
        nT, nT_b = A.alloc("nT", [128, KC, TT], BF16), Buf("nT")
        junk, jb = xn[:, 0, :], xn_b
        ss = A.alloc("ss", [128, 4], F32)
        rstd = A.alloc("rstd", [128, 4], F32)
        sb = Buf("ss")
        ss2 = A.alloc("ss2", [128, 4], F32)
        ss2_b = Buf("ss2")
        junk2, j2b = A.alloc("junk2", [128, 256], BF16), Buf("junk2")
        sg, sg_b = A.alloc("sg", [128, 512], F32), Buf("sg")
        gst, gst_b = A.alloc("gst", [128, D], F32), Buf("gst")
        glrT, glr_b = A.alloc("glrT", [16, TT], F32), Buf("glrT")
        zt, zt_b = sg, sg_b

        def two(name, shape, dt):
            return [A.alloc(name, shape, dt) for _ in range(2)], [Buf(name + "0"), Buf(name + "1")]

        qT, qT_b = two("qT", [128, 4, TT], F32)
        kT, kT_b = two("kT", [128, 4, TT], F32)
        ktm, ktm_b = two("ktm", [128, TB, 512], F32)
        vbf, vbf_b = two("vbf", [128, TB, D], BF16)
        ga, ga_b = two("ga", [128, TB, D], F32)
        Lt, Lt_b = two("Lt", [128, TB, 512], F32)
        E1, E1_b = A.alloc("E1", [128, 4, 128], F32), Buf("E1")
        E2, E2_b = A.alloc("E2", [128, 4, 128], F32), Buf("E2")
        E3, E3_b = A.alloc("E3", [128, 512], F32), Buf("E3")
        qin, qin_b = A.alloc("qin", [128, 4, 128], BF16), Buf("qin")
        kout, kout_b = A.alloc("kout", [128, 4, 128], BF16), Buf("kout")
        kst, kst_b = A.alloc("kst", [128, 512], BF16), Buf("kst")
        scT, scT_b = A.alloc("scT", [128, 4, 128], BF16), Buf("scT")
        osb, osb_b = A.alloc("osb", [128, D], F32), Buf("osb")

        def act(out, in_, func, reads, writes, **kw):
            P.op("act", lambda e: e.activation(out=out, in_=in_, func=func, **kw), reads, writes)

        def tt(eng, out, in0, in1, op, reads, writes):
            P.op(eng, lambda e: e.tensor_tensor(out=out, in0=in0, in1=in1, op=op), reads, writes)

        def stt(out, in0, scalar, in1, op0, op1, reads, writes):
            P.op("dve", lambda e: e.scalar_tensor_tensor(out=out, in0=in0, scalar=scalar, in1=in1, op0=op0, op1=op1), reads, writes)

        cpool = PsPool([0, 1, 2, 3])
        fpool = PsPool([4, 5, 6])

        def front(t):
            next_psf = fpool.next
            s = t % 2
            r0 = t * TT
            P.dma("sp", ht[:], h1_d[r0:r0 + TT, :].rearrange("(tb p) d -> p tb d", p=128), dsem("ld0"), writes=[ht_b])
            yield
            norm_transpose(ht, ht_b, TB, xn, xn_b, nT, nT_b, PK_NM, ss, rstd, sb, junk, jb)
            yield
            for (dst, dst_b, c0) in ((qT[s], qT_b[s], C_GQ), (kT[s], kT_b[s], C_GK)):
                for j in range(2):
                    ps, pb = next_psf()
                    items = []
                    for hh in range(2):
                        h = j * 2 + hh
                        for k in range(KC):
                            items.append((ps[:, hh * TT:(hh + 1) * TT], Wa[:, k, c0 + h * 128:c0 + (h + 1) * 128],
                                          nT[:, k, :], k == 0, k == KC - 1))
                    P.mms(items, reads=[wb, nT_b], writes=[pb])
                    act(dst[:, j * 2:j * 2 + 2, :], ps[:, :].rearrange("p (h t) -> p h t", h=2), AF.Copy, [pb], [dst_b])
                    yield
            ps, pb = next_psf()
            P.mm(ps[0:16, 0:TT], [(Wa[:, k, C_GLR:C_GLR + 16], nT[:, k, :]) for k in range(KC)], reads=[wb, nT_b], writes=[pb])
            act(glrT[:], ps[0:16, 0:TT], AF.Copy, [pb], [glr_b])
            yield
            for tb in range(TB):
                tok = slice(tb * 128, (tb + 1) * 128)
                ps, pb = next_psf()
                P.mms([(ps[:, :], glrT[:, tok], w2[:], True, False),
                       (ps[:, :], ones_r[:], b2[:], False, True)], reads=[glr_b, wb], writes=[pb])
                act(zt[:], ps[:, :], AF.Exp, [pb], [zt_b], scale=-1.0)
                act(Lt[s][:, tb, :], zt[:], AF.Ln, [zt_b], [Lt_b[s]], bias=1.0)
                yield
                ps, pb = next_psf()
                P.mm(ps[:, :], [(nT[:, k, tok], Wa[:, k, C_GK:C_GK + 512]) for k in range(KC)], reads=[wb, nT_b], writes=[pb])
                act(ktm[s][:, tb, :], ps[:, :], AF.Copy, [pb], [ktm_b[s]])
                yield
                for ch in range(2):
                    cs = slice(ch * 512, (ch + 1) * 512)
                    ps, pb = next_psf()
                    P.mm(ps[:, :], [(nT[:, k, tok], Wa[:, k, C_GV + ch * 512:C_GV + (ch + 1) * 512]) for k in range(KC)],
                         reads=[wb, nT_b], writes=[pb])
                    act(vbf[s][:, tb, cs], ps[:, :], AF.Copy, [pb], [vbf_b[s]])
                    yield
                for ch in range(2):
                    cs = slice(ch * 512, (ch + 1) * 512)
                    for (Wg_, cg, dst, dst_b) in ((Wa, C_GR + ch * 512, ga[s][:, tb, cs], ga_b[s]), (Wdg, ch * 512, gst[:, cs], gst_b)):
                        pg_, pgb_ = next_psf()
                        P.mm(pg_[:, :], [(nT[:, k, tok], Wg_[:, k, cg:cg + 512]) for k in range(KC)], reads=[wb, wb2, nT_b], writes=[pgb_])
                        act(dst, pg_[:, :], AF.Silu, [pgb_], [dst_b])
                yield
                for ch in range(2):
                    cs = slice(ch * 512, (ch + 1) * 512)
                    for (Wm_, dst, dst_b) in ((Wm, ga[s][:, tb, cs], ga_b[s]), (Wmb, gst[:, cs], gst_b)):
                        pm_, pmb_ = next_psf()
                        P.mm(pm_[:, :], [(nT[:, k, tok], Wm_[:, k, cs]) for k in range(KC)], reads=[wb, wb2, nT_b], writes=[pmb_])
                        act(sg[:], pm_[:, :], AF.Sigmoid, [pmb_], [sg_b])
                        tt("dve", dst, dst, sg[:], ALU.mult, [sg_b, dst_b], [dst_b])
                yield
                P.dma("sp", gb_d[r0 + tb * 128:r0 + (tb + 1) * 128, :], gst[:], dsem("stg"), reads=[gst_b])
                yield

        def chunks(t):
            s = t % 2
            r0 = t * TT
            for tb in range(TB):
                tok = slice(tb * 128, (tb + 1) * 128)
                ps, pb = cpool.bank(0)
                P.mms([(ps[:, h * 128:(h + 1) * 128], Lt[s][:, tb, h * 128:(h + 1) * 128], cst(K_UQ), True, True)
                       for h in range(4)], reads=[Lt_b[s], cb], writes=[pb])
                pd, pdb = cpool.bank(1)
                P.mm(pd[:, :], [(cst(K_US), Lt[s][:, tb, :])], reads=[Lt_b[s], cb], writes=[pdb])
                yield
                act(E1[:], ps[:, :].rearrange("p (h t) -> p h t", h=4), AF.Exp, [pb], [E1_b])
                act(E2[:], ps[:, :].rearrange("p (h t) -> p h t", h=4), AF.Exp, [pb], [E2_b], scale=-1.0)
                act(E3[:], pd[:, :], AF.Exp, [pdb], [E3_b])
                yield
                stt(qin[:], qT[s][:, :, tok], 128.0 ** -0.5, E1[:], ALU.mult, ALU.mult, [qT_b[s], E1_b], [qin_b])
                tt("dve", kout[:], kT[s][:, :, tok], E2[:], ALU.mult, [kT_b[s], E2_b], [kout_b])
                tt("pool", kst[:], ktm[s][:, tb, :], E3[:], ALU.mult, [ktm_b[s], E3_b], [kst_b])
                yield
                ps, pb = cpool.bank(0)
                P.mms([(ps[:, h * 128:(h + 1) * 128], kout[:, h, :], qin[:, h, :], True, True) for h in range(4)],
                      reads=[kout_b, qin_b], writes=[pb])
                yield
                tt("dve", scT[:], ps[:, :].rearrange("p (h t) -> p h t", h=4),
                   cst(K_CM).unsqueeze(1).to_broadcast([128, 4, 128]), ALU.mult, [pb, cb], [scT_b])
                yield
                po = [cpool.bank(2), cpool.bank(3)]
                for j in range(2):
                    items = []
                    for hh in range(2):
                        h = j * 2 + hh
                        items.append((po[j][0][:, hh * 256:(hh + 1) * 256], qin[:, h, :], Sb[:, h, :], True, False))
                        items.append((po[j][0][:, hh * 256:(hh + 1) * 256], scT[:, h, :], vbf[s][:, tb, h * 256:(h + 1) * 256], False, True))
                    P.mms(items, reads=[qin_b, Sb_b, scT_b, vbf_b[s]], writes=[po[j][1]])
                pu = [cpool.bank(0), cpool.bank(1)]
                for j in range(2):
                    items = []
                    for hh in range(2):
                        h = j * 2 + hh
                        items.append((pu[j][0][:, hh * 256:(hh + 1) * 256], kst[:, h * 128:(h + 1) * 128],
                                      vbf[s][:, tb, h * 256:(h + 1) * 256], True, True))
                    P.mms(items, reads=[kst_b, vbf_b[s]], writes=[pu[j][1]])
                yield
                for h in range(4):
                    j, hh = divmod(h, 2)
                    stt(S[:, h, :], S[:, h, :], E1[:, h, 127:128], pu[j][0][:, hh * 256:(hh + 1) * 256], ALU.mult, ALU.add,
                        [S_b, E1_b, pu[j][1]], [S_b])
                yield
                act(Sb[:], S[:], AF.Copy, [S_b], [Sb_b])
                yield
                for h in range(4):
                    j, hh = divmod(h, 2)
                    act(junk2[:], po[j][0][:, hh * 256:(hh + 1) * 256], AF.Square, [po[j][1]], [j2b, ss2_b],
                        accum_out=ss2[:, h:h + 1])
                yield
                act(ss2[:], ss2[:], AF.Ln, [ss2_b], [ss2_b], bias=EPS, scale=1.0 / 256.0)
                act(ss2[:], ss2[:], AF.Exp, [ss2_b], [ss2_b], scale=-0.5)
                yield
                for h in range(4):
                    j, hh = divmod(h, 2)
                    stt(osb[:, h * 256:(h + 1) * 256], po[j][0][:, hh * 256:(hh + 1) * 256], ss2[:, h:h + 1], ghn[:],
                        ALU.mult, ALU.mult, [po[j][1], ss2_b, wb], [osb_b])
                yield
                tt("pool", osb[:], osb[:], ga[s][:, tb, :], ALU.mult, [osb_b, ga_b[s]], [osb_b])
                P.dma("sp", ya_d[r0 + tb * 128:r0 + (tb + 1) * 128, :], osb[:], dsem("st0"), reads=[osb_b])
                yield

        def run_rr(gens):
            gens = list(gens)
            while gens:
                for g in list(gens):
                    try:
                        next(g)
                    except StopIteration:
                        gens.remove(g)

        run_rr([front(0)])
        for t in range(ntiles):
            gens = [chunks(t)]
            if t + 1 < ntiles:
                gens.append(front(t + 1))
            run_rr(gens)

    def dn_stage():
        P.barrier()
        A.reset(persist_mark)
        TT = 128
        ntiles = NT // TT
        O_BA = 3072
        Wb = A.alloc("Wb", [128, KC, 3088], BF16)
        wb = Buf("dnw")
        load_w(Wb, wb, win_d, C_DQ, C_DQ + 3072, "w0")
        for k in range(KC):
            P.dma("pool", Wb[:, k, 3072:3088], win_d[k * 128:(k + 1) * 128, C_DB:C_DB + 16], dsem("w0"), writes=[wb])
        dhn = A.alloc("dhn", [128, 128], F32)
        P.dma("sp", dhn[:], rows_d[:, RW_DHN:RW_DHN + 128].partition_broadcast(128), dsem("c0"), writes=[wb])
        nega = A.alloc("nega", [128, 8], F32)
        P.op("act", lambda e: e.activation(out=nega[:], in_=pk[:, PK_ALOG:PK_ALOG + 8], func=AF.Exp), reads=[cb], writes=[wb])
        P.op("dve", lambda e: e.tensor_scalar(out=nega[:], in0=nega[:], scalar1=-1.0, scalar2=None, op0=ALU.mult),
             reads=[wb], writes=[wb])

        def hb(name):
            return [Buf(name + "0"), Buf(name + "1")]

        def t3(name, dt=F32):
            return A.alloc(name, [128, 8, 128], dt), hb(name)

        S, S_b = t3("S")
        Sb, Sb_b = t3("Sb", BF16)
        for j in range(2):
            P.op("pool", lambda e, j=j: e.memset(S[:, j * 4:j * 4 + 4, :], 0.0), writes=[S_b[j]])
            P.op("pool", lambda e, j=j: e.memset(Sb[:, j * 4:j * 4 + 4, :], 0.0), writes=[Sb_b[j]])
        xc = A.alloc("xc", [128, 24, TT + 3], F32)
        xc_b = [Buf(f"xc{g}") for g in range(6)]
        P.op("pool", lambda e: e.memset(xc[:], 0.0), writes=xc_b)
        cv = A.alloc("cv", [128, 24, TT], F32)
        cvc_b = [Buf(f"cv{c}") for c in range(16)]
        cvg_b = [Buf("cvg4"), Buf("cvg5")]
        ctmp = [A.alloc("ctmp", [128, 4, TT], F32) for _ in range(2)]
        ctmp_b = [Buf("ctmp0"), Buf("ctmp1")]

        ht, ht_b = A.alloc("ht", [128, 1, D], F32), Buf("ht")
        yat, yat_b = A.alloc("yat", [128, D], F32), Buf("yat")
        gbt, gbt_b = A.alloc("gbt", [128, D], F32), Buf("gbt")
        xn, xn_b = A.alloc("xn", [128, 1, D], BF16), Buf("xn")
        nT, nT_b = A.alloc("nT", [128, KC, TT], BF16), Buf("nT")
        junk, jb = xn[:, 0, :], xn_b
        ss = A.alloc("ss", [128, 16], F32)
        rstd = A.alloc("rstd", [128, 16], F32)
        sb = Buf("ss")
        ss2 = A.alloc("ss2", [128, 8], F32)
        ss2_b = hb("ss2")
        sc2 = [A.alloc("sc", [128, 64], F32) for _ in range(2)]
        sc2_b = [Buf("sc0"), Buf("sc1")]
        qh, qh_b = t3("qh", BF16)
        kh, kh_b = t3("kh", BF16)
        vT, vT_b = t3("vTb", BF16)
        ktm, ktm_b = t3("ktm", BF16)
        vtm, vtm_b = t3("vtm", BF16)
        MG, MG_b = t3("MG")
        MG2, MG2_b = t3("MG2")
        EG, EG_b = t3("EG")
        Et, Et_b = t3("Et")
        En, En_b = t3("En")
        IDT = BF16 if INV_BF16 else F32
        Bm, Bm_b = t3("Bm", IDT)
        Bt, Bt_b = t3("Bt", IDT)
        Bm2, Bm2_b = t3("Bm2", IDT)
        Bt2, Bt2_b = t3("Bt2", IDT)
        RT, RT_b = t3("RT", IDT)
        rn = A.alloc("rn", [128, 16, 128], F32)
        rnq_bs, rnk_bs = hb("rnq"), hb("rnk")
        sq = A.alloc("sq", [128, 16, 128], BF16)
        sq_b = hb("sq")
        if INV_BF16:
            RTb, RTb_b = RT, RT_b
        else:
            RTb, RTb_b = t3("RTb", BF16)
        qkT, qkT_b = t3("qkT", BF16)
        vbt, vbt_b = t3("vbt", BF16)
        kbg, kbg_b = t3("kbg", BF16)
        kst, kst_b = t3("kst", BF16)
        qd, qd_b = t3("qd", BF16)
        wT, wT_b = t3("wT", BF16)
        us, us_b = t3("us")
        gbm = A.alias("gbm", [128, 8, 128], F32, A.last_off)
        gbm_b = us_b
        vn, vn_b = t3("vn", BF16)
        osb, osb_b = t3("osb")
        DG, DG_b = t3("DG")
        yo, yo_b = A.alloc("yo", [128, D], BF16), hb("yo")

        def act(out, in_, func, reads, writes, **kw):
            P.op("act", lambda e: e.activation(out=out, in_=in_, func=func, **kw), reads, writes)

        def tt(eng, out, in0, in1, op, reads, writes):
            P.op(eng, lambda e: e.tensor_tensor(out=out, in0=in0, in1=in1, op=op), reads, writes)

        def stt(out, in0, scalar, in1, op0, op1, reads, writes):
            P.op("dve", lambda e: e.scalar_tensor_tensor(out=out, in0=in0, scalar=scalar, in1=in1, op0=op0, op1=op1), reads, writes)

        def ts(eng, out, in0, s1, s2, op0, op1, reads, writes):
            if s2 is None:
                P.op(eng, lambda e: e.tensor_scalar(out=out, in0=in0, scalar1=s1, scalar2=None, op0=op0), reads, writes)
            else:
                P.op(eng, lambda e: e.tensor_scalar(out=out, in0=in0, scalar1=s1, scalar2=s2, op0=op0, op1=op1), reads, writes)

        def v3(ps):
            return ps[:, :].rearrange("p (h t) -> p h t", h=4)

        def bc4(sc, col0, j):
            return sc[:, col0 + j * 4:col0 + j * 4 + 4].unsqueeze(2).to_broadcast([128, 4, 128])

        def cbc(i):
            return cst(i).unsqueeze(1).to_broadcast([128, 4, 128])

        def cw(g, tap):
            return pk[:, PK_CONV + g * 16:PK_CONV + g * 16 + 16].rearrange("p (c j) -> p c j", j=4)[:, :, tap:tap + 1].to_broadcast([128, 4, TT])

        hpool = [PsPool([0, 1, 2]), PsPool([3, 4, 5])]
        fpool = PsPool([6])

        def front(t):
            next_psf = fpool.next
            r0 = t * TT
            P.dma("sp", ht[:, 0, :], h1_d[r0:r0 + TT, :], dsem("ld0"), writes=[ht_b])
            yield
            norm_transpose(ht, ht_b, 1, xn, xn_b, nT, nT_b, PK_NM, ss, rstd, sb, junk, jb)
            yield
            for g in range(6):
                ps, pb = next_psf()
                items = []
                for cc in range(4):
                    c = g * 4 + cc
                    for k in range(KC):
                        items.append((ps[:, cc * 128:(cc + 1) * 128], Wb[:, k, c * 128:(c + 1) * 128], nT[:, k, :],
                                      k == 0, k == KC - 1))
                P.mms(items, reads=[wb, nT_b], writes=[pb])
                act(xc[:, g * 4:g * 4 + 4, 3:3 + TT], v3(ps), AF.Copy, [pb], [xc_b[g]])
                yield
            for tap in range(4):
                for c in range(16):
                    g = c // 4
                    wcol = pk[:, PK_CONV + c * 4 + tap:PK_CONV + c * 4 + tap + 1]
                    if tap == 0:
                        ts("dve", cv[:, c, :], xc[:, c, 0:TT], wcol, None, ALU.mult, None, [xc_b[g], cb], [cvc_b[c]])
                    else:
                        stt(cv[:, c, :], xc[:, c, tap:tap + TT], wcol, cv[:, c, :], ALU.mult, ALU.add,
                            [xc_b[g], cb, cvc_b[c]], [cvc_b[c]])
                    if c % 4 == 3:
                        yield
            for gi, g in enumerate((4, 5)):
                gs = slice(g * 4, g * 4 + 4)
                tt("pool", cv[:, gs, :], xc[:, gs, 0:TT], cw(g, 0), ALU.mult, [xc_b[g], cb], [cvg_b[gi]])
            yield
            for tap in range(1, 4):
                for gi, g in enumerate((4, 5)):
                    gs = slice(g * 4, g * 4 + 4)
                    tt("pool", ctmp[gi][:], xc[:, gs, tap:tap + TT], cw(g, tap), ALU.mult, [xc_b[g], cb], [ctmp_b[gi]])
                    tt("pool", cv[:, gs, :], cv[:, gs, :], ctmp[gi][:], ALU.add, [ctmp_b[gi], cvg_b[gi]], [cvg_b[gi]])
                yield
            for g in range(6):
                gs = slice(g * 4, g * 4 + 4)
                cvb = cvc_b[g * 4:g * 4 + 4] if g < 4 else [cvg_b[g - 4]]
                P.op("pool", lambda e, gs=gs: e.tensor_copy(out=xc[:, gs, 0:3], in_=xc[:, gs, TT:TT + 3]),
                     reads=[xc_b[g]] + cvb, writes=[xc_b[g]])
            yield
            allcv = cvc_b + cvg_b
            act(cv[:], cv[:], AF.Silu, allcv, allcv)
            yield
            yield from scal(t)

        def scal(t):
            next_psf = fpool.next
            sc, sc_b = sc2[t % 2], sc2_b[t % 2]
            ps, pb = next_psf()
            P.mm(ps[:, 0:16], [(nT[:, k, :], Wb[:, k, O_BA:O_BA + 16]) for k in range(KC)], reads=[wb, nT_b], writes=[pb])
            act(sc[:, 0:8], ps[:, 0:8], AF.Exp, [pb], [sc_b], scale=-1.0)
            ts("dve", sc[:, 0:8], sc[:, 0:8], 1.0, None, ALU.add, None, [sc_b], [sc_b])
            P.op("dve", lambda e: e.reciprocal(out=sc[:, 0:8], in_=sc[:, 0:8]), reads=[sc_b], writes=[sc_b])
            yield
            tt("dve", sc[:, 48:56], ps[:, 8:16], pk[:, PK_DTB:PK_DTB + 8], ALU.add, [pb, cb, sc_b], [sc_b])
            yield
            act(sc[:, 48:56], sc[:, 48:56], AF.Exp, [sc_b], [sc_b])
            act(sc[:, 48:56], sc[:, 48:56], AF.Ln, [sc_b], [sc_b], bias=1.0)
            yield
            tt("dve", sc[:, 8:16], sc[:, 48:56], nega[:], ALU.mult, [sc_b, wb], [sc_b])
            ts("dve", sc[:, 24:32], sc[:, 0:8], -1.0, None, ALU.mult, None, [sc_b], [sc_b])
            yield
            ps, pb = next_psf()
            P.mms([(ps[:, 0:8], cst(K_U64), sc[:, 8:16], True, True),
                   (ps[:, 8:16], cst(K_USUF), sc[:, 8:16], True, True)], reads=[sc_b, cb], writes=[pb])
            yield
            act(sc[:, 16:24], ps[:, 0:8], AF.Copy, [pb, sc_b], [sc_b])
            act(sc[:, 56:64], ps[:, 0:8], AF.Exp, [pb, sc_b], [sc_b])
            act(sc[:, 40:48], ps[:, 8:16], AF.Exp, [pb, sc_b], [sc_b])
            yield
            tt("dve", sc[:, 32:40], sc[:, 56:64], sc[:, 0:8], ALU.mult, [sc_b], [sc_b])
            yield

        def half(t, j):
            next_psf = hpool[j].next
            hs = slice(j * 4, j * 4 + 4)
            r0 = t * TT
            sc, sc_b = sc2[t % 2], sc2_b[t % 2]
            cvq = cvc_b[j * 4:j * 4 + 4]
            cvk = cvc_b[8 + j * 4:8 + j * 4 + 4]
            cvv = [cvg_b[j]]
            qs, ks, vs = slice(j * 4, j * 4 + 4), slice(8 + j * 4, 12 + j * 4), slice(16 + j * 4, 20 + j * 4)
            rnq_b, rnk_b = rnq_bs[j], rnk_bs[j]
            sqq, sqk = slice(j * 8, j * 8 + 4), slice(j * 8 + 4, j * 8 + 8)
            act(sq[:, sqq, :], cv[:, qs, :], AF.Square, cvq, [sq_b[j]])
            act(sq[:, sqk, :], cv[:, ks, :], AF.Square, cvk, [sq_b[j]])
            P.op("pool", lambda e: e.tensor_copy(out=vT[:, hs, :], in_=cv[:, vs, :]), reads=cvv, writes=[vT_b[j]])
            yield
            for (sl, ssl, rb) in ((qs, sqq, rnq_b), (ks, sqk, rnk_b)):
                ps, pb = next_psf()
                P.mm(ps[:, :], [(ones_b[:], sq[:, ssl, :].rearrange("p h t -> p (h t)"))], reads=[sq_b[j], cb], writes=[pb])
                act(rn[:, sl, :], v3(ps), AF.Ln, [pb, rb], [rb], bias=EPS)
                act(rn[:, sl, :], rn[:, sl, :], AF.Exp, [rb], [rb], scale=-0.5)
                yield
            stt(qh[:, hs, :], cv[:, qs, :], 128.0 ** -0.5, rn[:, qs, :], ALU.mult, ALU.mult, cvq + [rnq_b], [qh_b[j]])
            tt("dve", kh[:, hs, :], cv[:, ks, :], rn[:, ks, :], ALU.mult, cvk + [rnk_b], [kh_b[j]])
            yield
            for (src, src_b, dst, dst_b) in ((kh, kh_b, ktm, ktm_b), (vT, vT_b, vtm, vtm_b)):
                pst, pstb = next_psb()

                def fn(e, src=src, pst=pst):
                    ins = None
                    for hh in range(4):
                        ins = e.transpose(out=pst[:, hh * 128:(hh + 1) * 128], in_=src[:, j * 4 + hh, :], identity=ident_b[:])
                    return ins
                P.op("pe", fn, reads=[src_b[j], cb], writes=[pstb])
                act(dst[:, hs, :], pst[:, 0:512].rearrange("p (h t) -> p h t", h=4), AF.Copy, [pstb], [dst_b[j]])
                yield
            tt("pool", vbt[:, hs, :], vtm[:, hs, :], bc4(sc, 0, j), ALU.mult, [vtm_b[j], sc_b], [vbt_b[j]])
            tt("pool", kbg[:, hs, :], ktm[:, hs, :], bc4(sc, 32, j), ALU.mult, [ktm_b[j], sc_b], [kbg_b[j]])
            tt("pool", kst[:, hs, :], ktm[:, hs, :], bc4(sc, 40, j), ALU.mult, [ktm_b[j], sc_b], [kst_b[j]])
            yield
            P.op("dve", lambda e: e.tensor_copy(out=gbm[:, hs, :], in_=bc4(sc, 8, j)), reads=[sc_b], writes=[gbm_b[j]])
            pg, pgb = next_psf()
            P.mms([(pg[:, hh * 128:(hh + 1) * 128], gbm[:, j * 4 + hh, :], cst(K_U64), True, True) for hh in range(4)],
                  reads=[gbm_b[j], cb], writes=[pgb])
            yield
            act(EG[:, hs, :], v3(pg), AF.Exp, [pgb], [EG_b[j]])
            tt("dve", MG[:, hs, :], v3(pg), cbc(K_NEGT), ALU.add, [pgb, cb], [MG_b[j]])
            stt(MG2[:, hs, :], v3(pg), -1.0, cbc(K_NEGS), ALU.mult, ALU.add, [pgb, cb], [MG2_b[j]])
            yield
            tt("dve", MG[:, hs, :], MG[:, hs, :], bc4(sc, 16, j), ALU.subtract, [MG_b[j], sc_b], [MG_b[j]])
            act(Et[:, hs, :], MG[:, hs, :], AF.Exp, [MG_b[j]], [Et_b[j]])
            tt("dve", MG2[:, hs, :], MG2[:, hs, :], bc4(sc, 16, j), ALU.add, [MG2_b[j], sc_b], [MG2_b[j]])
            act(En[:, hs, :], MG2[:, hs, :], AF.Exp, [MG2_b[j]], [En_b[j]])
            yield
            tt("pool", En[:, hs, :], En[:, hs, :], bc4(sc, 24, j), ALU.mult, [En_b[j], sc_b], [En_b[j]])
            tt("pool", qd[:, hs, :], qh[:, hs, :], EG[:, hs, :], ALU.mult, [qh_b[j], EG_b[j]], [qd_b[j]])
            yield
            pa, pab = next_psf()
            P.mms([(pa[:, hh * 128:(hh + 1) * 128], kh[:, j * 4 + hh, :], kh[:, j * 4 + hh, :], True, True) for hh in range(4)],
                  reads=[kh_b[j]], writes=[pab])
            pq, pqb = next_psf()
            P.mms([(pq[:, hh * 128:(hh + 1) * 128], kh[:, j * 4 + hh, :], qh[:, j * 4 + hh, :], True, True) for hh in range(4)],
                  reads=[kh_b[j], qh_b[j]], writes=[pqb])
            yield
            tt("dve", Bm[:, hs, :], v3(pa), En[:, hs, :], ALU.mult, [pab, En_b[j]], [Bm_b[j]])
            tt("dve", qkT[:, hs, :], v3(pq), Et[:, hs, :], ALU.mult, [pqb, Et_b[j]], [qkT_b[j]])
            yield
            if INV_BF16:
                pt, ptb = next_psb()
                ptv = pt[:, 0:512].rearrange("p (h t) -> p h t", h=4)
                idt = ident_b[:]
            else:
                pt, ptb = next_psf()
                ptv = v3(pt)
                idt = cst(K_ID)

            def fnt(e, pt=pt, idt=idt):
                ins = None
                for hh in range(4):
                    ins = e.transpose(out=pt[:, hh * 128:(hh + 1) * 128], in_=Bm[:, j * 4 + hh, :], identity=idt)
                return ins
            P.op("pe", fnt, reads=[Bm_b[j], cb], writes=[ptb])
            act(Bt[:, hs, :], ptv, AF.Copy, [ptb], [Bt_b[j]])
            tt("dve", RT[:, hs, :], ptv, cbc(K_ID), ALU.add, [ptb, cb], [RT_b[j]])
            yield
            Pc, Pc_b, PTc, PTc_b = Bm, Bm_b, Bt, Bt_b
            Pn, Pn_b, PTn, PTn_b = Bm2, Bm2_b, Bt2, Bt2_b
            for lvl in range(1, 6):
                last = lvl == 5
                pp, ppb = next_psf()
                P.mms([(pp[:, hh * 128:(hh + 1) * 128], PTc[:, j * 4 + hh, :], Pc[:, j * 4 + hh, :], True, True) for hh in range(4)],
                      reads=[Pc_b[j], PTc_b[j]], writes=[ppb])
                if not last:
                    p2, p2b = next_psf()
                    P.mms([(p2[:, hh * 128:(hh + 1) * 128], Pc[:, j * 4 + hh, :], PTc[:, j * 4 + hh, :], True, True) for hh in range(4)],
                          reads=[Pc_b[j], PTc_b[j]], writes=[p2b])
                yield
                act(Pn[:, hs, :], v3(pp), AF.Copy, [ppb], [Pn_b[j]])
                if not last:
                    act(PTn[:, hs, :], v3(p2), AF.Copy, [p2b], [PTn_b[j]])
                yield
                pr, prb = next_psf()
                P.mms([(pr[:, hh * 128:(hh + 1) * 128], Pn[:, j * 4 + hh, :], RT[:, j * 4 + hh, :], True, True) for hh in range(4)],
                      reads=[Pn_b[j], RT_b[j]], writes=[prb])
                yield
                tt("dve", RT[:, hs, :], RT[:, hs, :], v3(pr), ALU.add, [prb, RT_b[j]], [RT_b[j]])
                yield
                Pc, Pc_b, PTc, PTc_b, Pn, Pn_b, PTn, PTn_b = Pn, Pn_b, PTn, PTn_b, Pc, Pc_b, PTc, PTc_b
            if not INV_BF16:
                act(RTb[:, hs, :], RT[:, hs, :], AF.Copy, [RT_b[j]], [RTb_b[j]])
                yield
            pu, pub = next_psf()
            P.mms([(pu[:, hh * 128:(hh + 1) * 128], RTb[:, j * 4 + hh, :], vbt[:, j * 4 + hh, :], True, True) for hh in range(4)],
                  reads=[RTb_b[j], vbt_b[j]], writes=[pub])
            pw, pwb = next_psf()
            P.mms([(pw[:, hh * 128:(hh + 1) * 128], kbg[:, j * 4 + hh, :], RTb[:, j * 4 + hh, :], True, True) for hh in range(4)],
                  reads=[RTb_b[j], kbg_b[j]], writes=[pwb])
            yield
            act(us[:, hs, :], v3(pu), AF.Copy, [pub], [us_b[j]])
            act(wT[:, hs, :], v3(pw), AF.Copy, [pwb], [wT_b[j]])
            yield
            tt("pool", DG[:, hs, :], gbt[:, j * 512:(j + 1) * 512].rearrange("p (h t) -> p h t", h=4),
               dhn[:].unsqueeze(1).to_broadcast([128, 4, 128]), ALU.mult, [gbt_b, wb], [DG_b[j]])
            for c in range(2):
                rows = slice(c * 64, (c + 1) * 64)
                gcol = c * 64 + 63
                pa, pab = next_psf()
                P.mms([(pa[rows, hh * 128:(hh + 1) * 128], wT[:, j * 4 + hh, rows], Sb[:, j * 4 + hh, :], True, True) for hh in range(4)],
                      reads=[wT_b[j], Sb_b[j]], writes=[pab])
                yield
                tt("dve", vn[rows, hs, :], us[rows, hs, :], pa[rows, :].rearrange("p (h t) -> p h t", h=4), ALU.subtract,
                   [pab, us_b[j]], [vn_b[j]])
                yield
                po, pob = next_psf()
                items = []
                for hh in range(4):
                    h = j * 4 + hh
                    items.append((po[rows, hh * 128:(hh + 1) * 128], qd[:, h, rows], Sb[:, h, :], True, False))
                    items.append((po[rows, hh * 128:(hh + 1) * 128], qkT[rows, h, rows], vn[rows, h, :], False, True))
                P.mms(items, reads=[qd_b[j], Sb_b[j], qkT_b[j], vn_b[j]], writes=[pob])
                pS, pSb = next_psf()
                P.mms([(pS[:, hh * 128:(hh + 1) * 128], kst[rows, j * 4 + hh, :], vn[rows, j * 4 + hh, :], True, True) for hh in range(4)],
                      reads=[kst_b[j], vn_b[j]], writes=[pSb])
                tt("dve", S[:, hs, :], S[:, hs, :], EG[:, hs, gcol:gcol + 1].to_broadcast([128, 4, 128]), ALU.mult,
                   [S_b[j], EG_b[j]], [S_b[j]])
                yield
                act(osb[rows, hs, :], po[rows, :].rearrange("p (h t) -> p h t", h=4), AF.Copy, [pob], [osb_b[j]])
                tt("dve", S[:, hs, :], S[:, hs, :], v3(pS), ALU.add, [S_b[j], pSb], [S_b[j]])
                yield
                act(Sb[:, hs, :], S[:, hs, :], AF.Copy, [S_b[j]], [Sb_b[j]])
                yield
            tt("pool", us[:, hs, :], osb[:, hs, :], osb[:, hs, :], ALU.mult, [osb_b[j], us_b[j]], [us_b[j]])
            yield
            P.op("dve", lambda e: e.tensor_reduce(out=ss2[:, hs], in_=us[:, hs, :], axis=AX.X, op=ALU.add),
                 reads=[us_b[j]], writes=[ss2_b[j]])
            yield
            act(ss2[:, hs], ss2[:, hs], AF.Ln, [ss2_b[j]], [ss2_b[j]], bias=EPS, scale=1.0 / 128.0)
            act(ss2[:, hs], ss2[:, hs], AF.Exp, [ss2_b[j]], [ss2_b[j]], scale=-0.5)
            yield
            tt("dve", osb[:, hs, :], osb[:, hs, :], ss2[:, hs].unsqueeze(2).to_broadcast([128, 4, 128]), ALU.mult,
               [osb_b[j], ss2_b[j]], [osb_b[j]])
            yield
            tt("dve", osb[:, hs, :], osb[:, hs, :], DG[:, hs, :], ALU.mult, [osb_b[j], DG_b[j]], [osb_b[j]])
            yield
            tt("dve", yo[:, j * 512:(j + 1) * 512], osb[:, hs, :].rearrange("p h t -> p (h t)"), yat[:, j * 512:(j + 1) * 512],
               ALU.add, [osb_b[j], yat_b], [yo_b[j]])
            P.dma("sp", y_d[r0:r0 + TT, j * 512:(j + 1) * 512], yo[:, j * 512:(j + 1) * 512], dsem(f"st{j}"), reads=[yo_b[j]])
            yield

        def run_rr(gens):
            gens = list(gens)
            while gens:
                for g in list(gens):
                    try:
                        next(g)
                    except StopIteration:
                        gens.remove(g)

        def loads(t):
            r0 = t * TT
            P.dma("sp", yat[:], ya_d[r0:r0 + TT, :], dsem("lda0"), writes=[yat_b])
            P.dma("sp", gbt[:], gb_d[r0:r0 + TT, :], dsem("ldg0"), writes=[gbt_b])

        run_rr([front(0)])
        for t in range(ntiles):
            loads(t)
            h0, h1 = half(t, 0), half(t, 1)
            for _ in range(6):
                next(h0)
                next(h1)
            gens = [h0, h1]
            if t + 1 < ntiles:
                gens.append(front(t + 1))
            run_rr(gens)

    ffn_stage(x_d, h1_d, wg1_d, wu1_d, wd1_d, PK_N1, final=False, with_mix=False)
    if upto >= 2:
        gla_stage()
    if upto >= 3:
        dn_stage()
    if upto >= 4:
        ffn_stage(h1_d, out_d, wg2_d, wu2_d, wd2_d, PK_N2, final=True, with_mix=True)

    with nc.Block() as block:
        P.emit(eng_sems, block, list(dsems.values()))
    return nc


def make_consts():
    i = np.arange(128)
    same = (i[:, None] // 64) == (i[None, :] // 64)
    c = np.zeros((8, 128, 128), np.float32)
    c[K_ID] = np.eye(128)
    c[K_UQ] = np.where(i[:, None] <= i[None, :], -1.0 / 16.0, 0.0)
    c[K_US] = np.where(i[:, None] > i[None, :], -1.0 / 16.0, 0.0)
    c[K_CM] = np.where(i[:, None] <= i[None, :], 1.0, 0.0)
    c[K_U64] = np.where((i[:, None] <= i[None, :]) & same, 1.0, 0.0)
    c[K_USUF] = np.where((i[:, None] > i[None, :]) & same, 1.0, 0.0)
    c[K_NEGT] = np.where((i[None, :] >= i[:, None]) & same, 0.0, NEG)
    c[K_NEGS] = np.where((i[None, :] < i[:, None]) & same, 0.0, NEG)
    return np.ascontiguousarray(c.transpose(1, 0, 2).reshape(128, 8 * 128))


def pack_params(inp):
    f = lambda a: np.asarray(a, np.float32)
    pk = np.zeros((128, PK_END), np.float32)
    pk[:, PK_N1:PK_N1 + 8] = f(inp["ffn1_norm"])[0].reshape(8, 128).T
    pk[:, PK_NM:PK_NM + 8] = f(inp["mix_norm"])[0].reshape(8, 128).T
    pk[:, PK_N2:PK_N2 + 8] = f(inp["ffn2_norm"])[0].reshape(8, 128).T
    cw = f(inp["conv_w"])[0]
    pk[:, PK_CONV:PK_CONV + 96] = cw.reshape(4, 24, 128).transpose(2, 1, 0).reshape(128, 96)
    pk[:, PK_ALOG:PK_ALOG + 8] = f(inp["dn_a_log"])[0][None, :]
    pk[:, PK_DTB:PK_DTB + 8] = f(inp["dn_dt_bias"])[0][None, :]
    rows = np.zeros((1, RW_END), np.float32)
    rows[0, RW_B2:RW_B2 + 512] = f(inp["b_gla_gate"])[0]
    rows[0, RW_GHN:RW_GHN + 256] = f(inp["gla_head_norm"])[0]
    rows[0, RW_DHN:RW_DHN + 128] = f(inp["dn_head_norm"])[0]
    rows[0, RW_FIN:RW_FIN + 1024] = f(inp["final_norm"])
    return pk, rows


def shared_maps(inp):
    f = lambda a: np.ascontiguousarray(np.asarray(a, np.float32))
    pk, rows = pack_params(inp)
    return {
        "wg1": f(inp["ffn1_w_gate"][0]), "wu1": f(inp["ffn1_w_up"][0]), "wd1": f(inp["ffn1_w_down"][0]),
        "wg2": f(inp["ffn2_w_gate"][0]), "wu2": f(inp["ffn2_w_up"][0]), "wd2": f(inp["ffn2_w_down"][0]),
        "w_in": f(inp["w_in"][0]), "w_out": f(inp["w_out"][0]), "w2": f(inp["w_gla_gate"][0]),
        "consts": make_consts(), "pk": pk, "rows": rows,
    }


def kernel(**inputs):
    x = np.asarray(inputs["x"], np.float32)
    shared = shared_maps(inputs)
    NT = SEQ
    nc = build_program(NT)
    in_maps = []
    for c in range(N_CORES):
        m = dict(shared)
        m["x"] = np.ascontiguousarray(x[c % BATCH])
        in_maps.append(m)
    res = run_bass_kernel_spmd(nc, in_maps, core_ids=list(range(N_CORES)))
    out = np.stack([np.asarray(res.results[b]["out"], np.float32) for b in range(BATCH)], axis=0)
    return out
```
